# Optimizing a Trainium2 kernel written in Bass

```python
import math
import functools
import jax
import jax.numpy as jnp
from jax import lax
import numpy as np

D_MODEL = 1024
BATCH = 16
SEQ = 256
DEPTH = 4
DEC_BATCH = 4
DEC_SEQ = 1024
PAST_LEN = 256

GRID_W = 64
N_MIXERS = 4
N_MLSTM = (DEPTH + 3) // 4
N_CONV = (DEPTH + 2) // 4
N_POOL = (DEPTH + 1) // 4
N_FOURIER = DEPTH // 4
ML_HEADS = 8
ML_DQK = D_MODEL // 16
ML_DHV = D_MODEL // 8
ML_CHUNK = 64
ML_QK = ML_HEADS * ML_DQK
ML_V = ML_HEADS * ML_DHV
ML_IN = 2 * ML_QK + 2 * ML_V
N_GROUPS = 4
GROUP_W = D_MODEL // N_GROUPS
POOL_WINDOWS = (2, 4, 8, 16)
CONV_W = 3
D_FF = 4 * D_MODEL
ALPHA = (2.0 * DEPTH) ** 0.25
BETA = (8.0 * DEPTH) ** -0.25
LN_EPS = 1e-5
F32 = jnp.float32

kernel_name = "hybrid_mlstm_conv_pool_fourier_flow_step"


def layer_norm(x, g, b):
    xf = x.astype(F32)
    mu = jnp.mean(xf, axis=-1, keepdims=True)
    var = jnp.mean(jnp.square(xf - mu), axis=-1, keepdims=True)
    return ((xf - mu) * lax.rsqrt(var + LN_EPS) * g.astype(F32) + b.astype(F32)).astype(x.dtype)


def grid_pos_embed(rows, dtype):
    rr, cc = jnp.meshgrid(jnp.arange(rows, dtype=F32), jnp.arange(GRID_W, dtype=F32), indexing="ij")
    quarter = D_MODEL // 4
    omega = 1.0 / (10000.0 ** (jnp.arange(quarter, dtype=F32) / quarter))
    er = rr.reshape(-1, 1) * omega
    ec = cc.reshape(-1, 1) * omega
    return jnp.concatenate([jnp.sin(er), jnp.cos(er), jnp.sin(ec), jnp.cos(ec)], axis=-1).astype(dtype)


def mlstm_scan(q, k, v, ig, lf, C0, n0, m0):
    B, S, H, _ = q.shape
    L = math.gcd(S, ML_CHUNK)
    NC = S // L
    qc = q.reshape(B, NC, L, H, ML_DQK).transpose(1, 0, 3, 2, 4)
    kc = k.reshape(B, NC, L, H, ML_DQK).transpose(1, 0, 3, 2, 4)
    vc = v.reshape(B, NC, L, H, ML_DHV).transpose(1, 0, 3, 2, 4)
    ic = ig.reshape(B, NC, L, H).transpose(1, 0, 3, 2)
    fc = lf.reshape(B, NC, L, H).transpose(1, 0, 3, 2)
    mask = jnp.tril(jnp.ones((L, L), dtype=bool))

    def step(carry, inp):
        C, n, m = carry
        qq, kk, vv, ii, ff = inp
        b = jnp.cumsum(ff, axis=-1)
        dmat = jnp.where(mask, b[..., :, None] - b[..., None, :] + ii[..., None, :], -jnp.inf)
        inter = b + m[..., None]
        mt = jnp.maximum(inter, jnp.max(dmat, axis=-1))
        s = jnp.einsum("bhtd,bhsd->bhts", qq, kk) * jnp.exp(dmat - mt[..., None])
        a = jnp.exp(inter - mt)
        num = jnp.einsum("bhts,bhsv->bhtv", s, vv) + a[..., None] * jnp.einsum("bhtd,bhdv->bhtv", qq, C)
        den = jnp.sum(s, axis=-1) + a * jnp.einsum("bhtd,bhd->bht", qq, n)
        h = num / jnp.maximum(jnp.abs(den), jnp.exp(-mt))[..., None]
        m_new = mt[..., -1]
        b_last = b[..., -1]
        w = jnp.exp(b_last[..., None] - b + ii - m_new[..., None])
        dec = jnp.exp(b_last + m - m_new)
        C_new = dec[..., None, None] * C + jnp.einsum("bhs,bhsd,bhsv->bhdv", w, kk, vv)
        n_new = dec[..., None] * n + jnp.einsum("bhs,bhsd->bhd", w, kk)
        return (C_new, n_new, m_new), h

    (C, n, m), h = lax.scan(step, (C0, n0, m0), (qc, kc, vc, ic, fc))
    h = h.transpose(1, 0, 3, 2, 4).reshape(B, S, H, ML_DHV)
    return h, C, n, m


def mlstm_mixer(h, w_in, w_gate, b_gate, norm_g, w_out, C0, n0, m0):
    B, S, _ = h.shape
    q, k, v, o = jnp.split(h @ w_in, [ML_QK, 2 * ML_QK, 2 * ML_QK + ML_V], axis=-1)
    q = q.astype(F32).reshape(B, S, ML_HEADS, ML_DQK)
    k = k.astype(F32).reshape(B, S, ML_HEADS, ML_DQK) * (ML_DQK ** -0.5)
    v = v.astype(F32).reshape(B, S, ML_HEADS, ML_DHV)
    g = (h @ w_gate + b_gate).astype(F32).reshape(B, S, 2, 2, ML_HEADS)
    hs, Cs, ns, ms = [], [], [], []
    for d in range(2):
        seqs = (q, k, v, g[:, :, d, 0], jax.nn.log_sigmoid(g[:, :, d, 1]))
        if d == 1:
            seqs = tuple(jnp.flip(a, axis=1) for a in seqs)
        hd, Cd, nd, md = mlstm_scan(*seqs, C0[:, d].astype(F32), n0[:, d].astype(F32), m0[:, d].astype(F32))
        hs.append(jnp.flip(hd, axis=1) if d == 1 else hd)
        Cs.append(Cd)
        ns.append(nd)
        ms.append(md)
    hsum = hs[0] + hs[1]
    mu = jnp.mean(hsum, axis=-1, keepdims=True)
    var = jnp.mean(jnp.square(hsum - mu), axis=-1, keepdims=True)
    hn = (hsum - mu) * lax.rsqrt(var + LN_EPS) * norm_g.astype(F32).reshape(ML_HEADS, ML_DHV)
    y = (hn.reshape(B, S, ML_V) * jax.nn.sigmoid(o.astype(F32))).astype(h.dtype) @ w_out
    return y, jnp.stack(Cs, axis=1), jnp.stack(ns, axis=1), jnp.stack(ms, axis=1)


def shortconv_mixer(h, w_in, conv_w, w_out):
    bg, cg, u = jnp.split(h @ w_in, 3, axis=-1)
    cu = cg * u
    conv = lax.conv_general_dilated(cu, conv_w[:, None, :].astype(cu.dtype), window_strides=(1,),
                                    padding=((1, 1),), dimension_numbers=("NWC", "WIO", "NWC"),
                                    feature_group_count=D_MODEL)
    return (bg * conv) @ w_out


def pool_mixer(h, w, b, scale):
    B, S, _ = h.shape
    hg = h.astype(F32).reshape(B, S, N_GROUPS, GROUP_W)
    cs = jnp.pad(jnp.cumsum(hg, axis=1), ((0, 0), (1, 0), (0, 0), (0, 0)))
    t = jnp.arange(S)
    outs = []
    for gi, win in enumerate(POOL_WINDOWS):
        lo = jnp.clip(t - win // 2, 0, S)
        hi = jnp.clip(t + win - win // 2, 0, S)
        cs_g = cs[:, :, gi]
        mean = (cs_g[:, hi] - cs_g[:, lo]) / (hi - lo).astype(F32)[None, :, None]
        outs.append(mean - hg[:, :, gi])
    p = jnp.stack(outs, axis=2).astype(h.dtype)
    y = jnp.einsum("bsgc,gcd->bsgd", p, w) + b
    return y.reshape(B, S, D_MODEL) * scale


def fourier_mixer(h, w, b):
    B, S, _ = h.shape
    hg = h.astype(F32).reshape(B, S, N_GROUPS, GROUP_W)
    f = jnp.real(jnp.fft.fft2(hg, axes=(1, 3), norm="ortho"))
    return f.reshape(B, S, D_MODEL).astype(h.dtype) @ w + b


def sq_relu_mlp(h, w1, w2):
    return jnp.square(jax.nn.relu(h @ w1)) @ w2


def trunk(x, cond, st_C, st_n, st_m, w_mod, b_mod, ln_g, ln_b, mlp_w1, mlp_w2,
          ml_w_in, ml_w_gate, ml_b_gate, ml_norm_g, ml_w_out,
          sc_w_in, sc_conv_w, sc_w_out, pl_w, pl_b, pl_scale, ft_w_out, ft_b_out):
    s = jax.nn.silu(cond)
    fin_C, fin_n, fin_m = [], [], []
    for i in range(DEPTH):
        kind, j = i % N_MIXERS, i // N_MIXERS
        mod = (s @ w_mod[i] + b_mod[i])[:, None, :]
        sh1, sc1, g1, sh2, sc2, g2 = jnp.split(mod, 6, axis=-1)
        h = x * (1.0 + sc1) + sh1
        if kind == 0:
            y, Cf, nf, mf = mlstm_mixer(h, ml_w_in[j], ml_w_gate[j], ml_b_gate[j], ml_norm_g[j], ml_w_out[j],
                                        st_C[:, j], st_n[:, j], st_m[:, j])
            fin_C.append(Cf)
            fin_n.append(nf)
            fin_m.append(mf)
        elif kind == 1:
            y = shortconv_mixer(h, sc_w_in[j], sc_conv_w[j], sc_w_out[j])
        elif kind == 2:
            y = pool_mixer(h, pl_w[j], pl_b[j], pl_scale[j])
        else:
            y = fourier_mixer(h, ft_w_out[j], ft_b_out[j])
        x = layer_norm(ALPHA * x + g1 * y, ln_g[i, 0], ln_b[i, 0])
        h = x * (1.0 + sc2) + sh2
        x = layer_norm(ALPHA * x + g2 * sq_relu_mlp(h, mlp_w1[i], mlp_w2[i]), ln_g[i, 1], ln_b[i, 1])
    return x, jnp.stack(fin_C, axis=1), jnp.stack(fin_n, axis=1), jnp.stack(fin_m, axis=1)


def setup_inputs(seed: int = 0) -> dict:
    key = jax.random.key(seed)
    ks = jax.random.split(key, 26)
    D = D_MODEL

    def nrm(k, shape, scale):
        return jax.random.normal(k, shape, F32) * scale

    gate_noise = nrm(ks[15], (N_MLSTM, 2, 2, ML_HEADS), 0.1)
    gate_offset = jnp.array([0.0, 3.0], F32)[None, None, :, None]
    return {
        "x_prompt": nrm(ks[0], (BATCH, SEQ, D), 1.0),
        "x_sample": nrm(ks[1], (DEC_BATCH, DEC_SEQ, D), 1.0),
        "state_C": nrm(ks[2], (DEC_BATCH, N_MLSTM, 2, ML_HEADS, ML_DQK, ML_DHV), 0.1),
        "state_n": nrm(ks[3], (DEC_BATCH, N_MLSTM, 2, ML_HEADS, ML_DQK), 0.1),
        "state_m": nrm(ks[4], (DEC_BATCH, N_MLSTM, 2, ML_HEADS), 1.0),
        "c": nrm(ks[5], (DEC_BATCH, D), 1.0),
        "c_ctx": nrm(ks[6], (D,), 1.0),
        "w_mod": nrm(ks[7], (DEPTH, D, 6 * D), 0.5 * D ** -0.5),
        "b_mod": nrm(ks[8], (DEPTH, 6 * D), 0.02),
        "ln_g": 1.0 + nrm(ks[9], (DEPTH, 2, D), 0.02),
        "ln_b": nrm(ks[10], (DEPTH, 2, D), 0.02),
        "mlp_w1": nrm(ks[11], (DEPTH, D, D_FF), D ** -0.5),
        "mlp_w2": nrm(ks[12], (DEPTH, D_FF, D), BETA * D_FF ** -0.5),
        "ml_w_in": nrm(ks[13], (N_MLSTM, D, ML_IN), D ** -0.5),
        "ml_w_gate": nrm(ks[14], (N_MLSTM, D, 4 * ML_HEADS), D ** -0.5),
        "ml_b_gate": (gate_noise + gate_offset).reshape(N_MLSTM, 4 * ML_HEADS),
        "ml_norm_g": 1.0 + nrm(ks[16], (N_MLSTM, ML_V), 0.02),
        "ml_w_out": nrm(ks[17], (N_MLSTM, ML_V, D), BETA * ML_V ** -0.5),
        "sc_w_in": nrm(ks[18], (N_CONV, D, 3 * D), D ** -0.5),
        "sc_conv_w": nrm(ks[19], (N_CONV, CONV_W, D), CONV_W ** -0.5),
        "sc_w_out": nrm(ks[20], (N_CONV, D, D), BETA * D ** -0.5),
        "pl_w": nrm(ks[21], (N_POOL, N_GROUPS, GROUP_W, GROUP_W), BETA * GROUP_W ** -0.5),
        "pl_b": nrm(ks[22], (N_POOL, N_GROUPS, GROUP_W), 0.02),
        "pl_scale": 1.0 + nrm(ks[23], (N_POOL, D), 0.1),
        "ft_w_out": nrm(ks[24], (N_FOURIER, D, D), BETA * D ** -0.5),
        "ft_b_out": nrm(ks[25], (N_FOURIER, D), 0.02),
    }


def reference(x_prompt, x_sample, state_C, state_n, state_m, c, c_ctx,
              w_mod, b_mod, ln_g, ln_b, mlp_w1, mlp_w2,
              ml_w_in, ml_w_gate, ml_b_gate, ml_norm_g, ml_w_out,
              sc_w_in, sc_conv_w, sc_w_out, pl_w, pl_b, pl_scale, ft_w_out, ft_b_out):
    run = functools.partial(trunk, w_mod=w_mod, b_mod=b_mod, ln_g=ln_g, ln_b=ln_b,
                            mlp_w1=mlp_w1, mlp_w2=mlp_w2, ml_w_in=ml_w_in, ml_w_gate=ml_w_gate,
                            ml_b_gate=ml_b_gate, ml_norm_g=ml_norm_g, ml_w_out=ml_w_out,
                            sc_w_in=sc_w_in, sc_conv_w=sc_conv_w, sc_w_out=sc_w_out,
                            pl_w=pl_w, pl_b=pl_b, pl_scale=pl_scale, ft_w_out=ft_w_out, ft_b_out=ft_b_out)
    bp = x_prompt.shape[0]
    zC = jnp.zeros((bp, N_MLSTM, 2, ML_HEADS, ML_DQK, ML_DHV), F32)
    zn = jnp.zeros((bp, N_MLSTM, 2, ML_HEADS, ML_DQK), F32)
    zm = jnp.zeros((bp, N_MLSTM, 2, ML_HEADS), F32)
    y_prompt, new_C, new_n, new_m = run(x_prompt, c_ctx[None, :], zC, zn, zm)
    rows = x_sample.shape[1] // GRID_W
    xs = x_sample + grid_pos_embed(rows, x_sample.dtype)[None]
    y_sample, _, _, _ = run(xs, c, state_C, state_n, state_m)
    return (y_prompt, y_sample, new_C.astype(x_prompt.dtype), new_n.astype(x_prompt.dtype), new_m.astype(x_prompt.dtype))
```

```python
import contextlib
import numpy as np
import ml_dtypes
import concourse.bass as bass
import concourse.mybir as mybir
from concourse.bass_utils import run_bass_kernel_spmd

F32 = mybir.dt.float32
BF16 = mybir.dt.bfloat16
AF = mybir.ActivationFunctionType
ALU = mybir.AluOpType
AX = mybir.AxisListType

D = 1024
T = 1024
DEPTH = 4
ALPHA = 8.0 ** 0.25
LN_EPS = 1e-5
RING = 3
UNITC = 8192
ENGS = ["sync", "tensor", "vector", "scalar", "gpsimd"]

CFG = {"layers": [0, 1, 2, 3], "mixers": {0, 1, 2, 3}, "mlp": True, "same_engine_sync": True}


class Prog:
    def __init__(self, nc, n_dma_sems=28, same_engine_sync=True):
        self.nc = nc
        self.stack = contextlib.ExitStack()
        self.streams = {e: [] for e in ENGS}
        self.count = {e: 0 for e in ENGS}
        self.sem = {e: self.stack.enter_context(nc.semaphore("prog_" + e)) for e in ENGS}
        self.dsem = [self.stack.enter_context(nc.semaphore("dma_%d" % i)) for i in range(n_dma_sems)]
        self.dcount = [0] * n_dma_sems
        self.dnext = 0
        self.dnext_sw = 0
        self.res = {}
        self.fences = {}
        self.waited = {e: {} for e in ENGS}
        self.same_engine_sync = same_engine_sync
        self.final_events = []
        self.last_nonchain = {}

    def sb(self, name, shape, dtype):
        return self.stack.enter_context(self.nc.sbuf_tensor("S_" + name, shape, dtype))

    def ps(self, name, shape, dtype=F32):
        return self.stack.enter_context(self.nc.psum_tensor("P_" + name, shape, dtype))

    def _state(self, key):
        st = self.res.get(key)
        if st is None and isinstance(key, tuple) and key[0] in self.fences:
            st = {"w": None, "r": list(self.fences[key[0]]), "fw": list(self.fences[key[0]])}
            self.res[key] = st
        return st

    def fence(self, region):
        evs = list(self.fences.get(region, []))
        for key in [k for k in self.res if isinstance(k, tuple) and k[0] == region]:
            st = self.res.pop(key)
            if st["w"] is not None:
                evs.append(st["w"])
            evs.extend(st["r"])
        best = {}
        for key, val, src in evs:
            if key not in best or best[key][1] < val:
                best[key] = (key, val, src)
        self.fences[region] = list(best.values())

    def _deps(self, eng, reads, writes, chain=False):
        evs = []
        for r in reads:
            st = self._state(r)
            if st and st["w"] is not None:
                evs.append(st["w"])
            if st and st.get("fw"):
                evs.extend(st["fw"])
        for w in writes:
            st = self._state(w)
            if st:
                if st["w"] is not None:
                    evs.append(st["w"])
                evs.extend(st["r"])
        best = {}
        for key, val, src in evs:
            if src == eng and (chain or not self.same_engine_sync):
                continue
            if self.waited[eng].get(key, 0) >= val:
                continue
            best[key] = max(best.get(key, 0), val)
        for k, v in best.items():
            self.waited[eng][k] = v
        return list(best.items())

    def _semh(self, key):
        return self.sem[key] if isinstance(key, str) else self.dsem[key]

    def _commit(self, ev, reads, writes):
        for r in reads:
            st = self._state(r)
            if st is None:
                st = {"w": None, "r": []}
                self.res[r] = st
            st["r"].append(ev)
            if len(st["r"]) > 24:
                best = {}
                for key, val, src in st["r"]:
                    if key not in best or best[key][1] < val:
                        best[key] = (key, val, src)
                st["r"] = list(best.values())
        for w in writes:
            self.res[w] = {"w": ev, "r": []}

    def op(self, eng, fn, reads=(), writes=(), chain=False):
        waits = self._deps(eng, reads, writes, chain)
        if chain:
            ln = self.last_nonchain.get(eng, 0)
            if ln > 0 and self.waited[eng].get(eng, 0) < ln:
                self.waited[eng][eng] = ln
                waits.append((eng, ln))
        else:
            self.last_nonchain[eng] = self.count[eng] + 1
        self.count[eng] += 1
        ev = (eng, self.count[eng], eng)
        self.streams[eng].append((waits, fn, (eng, 1)))
        self._commit(ev, reads, writes)
        return ev

    def dma(self, q, fn, reads=(), writes=(), final=False):
        nsw = 8
        if q == "gpsimd":
            j = self.dnext_sw
            self.dnext_sw = (self.dnext_sw + 1) % nsw
        else:
            j = nsw + self.dnext
            self.dnext = (self.dnext + 1) % (len(self.dsem) - nsw)
        waits = self._deps(q, reads, writes)
        if self.dcount[j] > 0 and self.waited[q].get(j, 0) < self.dcount[j]:
            self.waited[q][j] = self.dcount[j]
            waits.append((j, self.dcount[j]))
        self.dcount[j] += 16
        ev = (j, self.dcount[j], "dma")
        self.streams[q].append((waits, fn, (j, 16)))
        self._commit(ev, reads, writes)
        if final:
            self.final_events.append(ev)
        return ev

    def finish(self):
        for q in ("sync",):
            waits = []
            for key, val, _ in self.final_events:
                if self.waited[q].get(key, 0) >= val:
                    continue
                self.waited[q][key] = val
                waits.append((key, val))
            self.streams[q].append((waits, None, None))

    def build(self):
        nc = self.nc
        with nc.Block() as block:
            def mk(ename):
                def body(e):
                    for waits, fn, inc in self.streams[ename]:
                        for key, val in waits:
                            e.wait_ge(self._semh(key), val)
                        if fn is not None:
                            ins = fn(e)
                            ins.then_inc(self._semh(inc[0]), inc[1])
                return body
            block.sync(mk("sync"))
            block.tensor(mk("tensor"))
            block.vector(mk("vector"))
            block.scalar(mk("scalar"))
            block.gpsimd(mk("gpsimd"))
        self.stack.close()


def fm(v):
    return np.ascontiguousarray(np.asarray(v, np.float32).reshape(8, 128).T)


def kmajor(W, n0, ncols=1024):
    W = np.asarray(W, np.float32)
    blk = W[:, n0:n0 + ncols].reshape(8, 128, ncols).transpose(1, 0, 2).reshape(128, 8 * ncols)
    out = np.zeros((128, UNITC), np.float32)
    out[:, :8 * ncols] = blk
    return out


def w2strip(W2, s):
    W2 = np.asarray(W2, np.float32)
    return np.ascontiguousarray(W2[:, s * 256:(s + 1) * 256].reshape(32, 128, 256).transpose(1, 0, 2).reshape(128, UNITC))


VC = {}
_off = 0
for _name, _n in [("bmod", 4 * 48), ("lng", 64), ("lnb", 64), ("convw", 24), ("plb", 8), ("pls", 8), ("ftb", 8),
                  ("flag", 1), ("bflag", 1), ("gbi", 1), ("ngbf", 1), ("m0", 1), ("mlng", 8)]:
    VC[_name] = _off
    _off += _n
NV = _off

CB = {}
_off = 0
for _name, _n in [("ident", 128), ("ones", 128), ("maskF", 128), ("maskB", 128), ("cwsw", 1024)]:
    CB[_name] = _off
    _off += _n
NCB = _off

CF = {}
_off = 0
for _name, _n in [("identf", 128), ("rs1f", 1024), ("rsmf", 1024), ("dmask", 16), ("ones128", 128)]:
    CF[_name] = _off
    _off += _n
NCF = _off


def unit_plan():
    plan = []
    for i in range(DEPTH):
        kind = i % 4
        if kind == 0:
            plan += [("ml_gate", 1024), ("ml_v", UNITC), ("ml_qk", UNITC), ("ml_qk2", UNITC), ("ml_o", UNITC), ("ml_out", UNITC)]
        elif kind == 1:
            plan += [("sc_u", UNITC), ("sc_cg", UNITC), ("sc_bg", UNITC), ("sc_out", UNITC)]
        elif kind == 2:
            plan += [("pl_w", 2048)]
        else:
            plan += [("ft_out", UNITC)]
        plan += [("w1_%d_%d" % (i, b), UNITC) for b in range(4)]
        plan += [("w2_%d_%d" % (i, s), UNITC) for s in range(4)]
        plan += [("w2b_%d_%d" % (i, s), UNITC) for s in range(4)]
    return plan


def gate_layout(w_gate, b_gate):
    wg = np.asarray(w_gate, np.float32).reshape(1024, 2, 2, 8)
    bg = np.asarray(b_gate, np.float32).reshape(2, 2, 8)
    G = np.zeros((1024, 128), np.float32)
    bi = np.zeros((128,), np.float32)
    bf = np.full((128,), 30.0, np.float32)
    for d in range(2):
        G[:, 32 * d:32 * d + 8] = wg[:, d, 0, :]
        G[:, 64 + 32 * d:64 + 32 * d + 8] = wg[:, d, 1, :]
        bi[32 * d:32 * d + 8] = bg[d, 0]
        bf[32 * d:32 * d + 8] = bg[d, 1]
    return G, bi, bf


def build_wall(inp):
    units = []
    for name, ncols in unit_plan():
        if name == "ml_gate":
            G, _, _ = gate_layout(inp["ml_w_gate"][0], inp["ml_b_gate"][0])
            units.append(kmajor(G, 0, 128))
        elif name in ("ml_qk", "ml_qk2"):
            units.append(kmajor(inp["ml_w_in"][0], 0))
        elif name == "ml_v":
            units.append(kmajor(inp["ml_w_in"][0], 1024))
        elif name == "ml_o":
            units.append(kmajor(inp["ml_w_in"][0], 2048))
        elif name == "ml_out":
            units.append(kmajor(inp["ml_w_out"][0], 0))
        elif name == "sc_u":
            units.append(kmajor(inp["sc_w_in"][0], 2048))
        elif name == "sc_cg":
            units.append(kmajor(inp["sc_w_in"][0], 1024))
        elif name == "sc_bg":
            units.append(kmajor(inp["sc_w_in"][0], 0))
        elif name == "sc_out":
            units.append(kmajor(inp["sc_w_out"][0], 0))
        elif name == "pl_w":
            w = np.asarray(inp["pl_w"][0], np.float32)
            blk = w.reshape(4, 2, 128, 256).transpose(2, 0, 1, 3).reshape(128, 2048)
            u = np.zeros((128, UNITC), np.float32)
            u[:, :2048] = blk
            units.append(u)
        elif name == "ft_out":
            units.append(kmajor(inp["ft_w_out"][0], 0))
        elif name.startswith("w1_") or name.startswith("w1b_"):
            _, i, b = name.split("_")
            units.append(kmajor(inp["mlp_w1"][int(i)], int(b) * 1024))
        elif name.startswith("w2_") or name.startswith("w2b_"):
            _, i, s = name.split("_")
            units.append(w2strip(inp["mlp_w2"][int(i)], int(s)))
        else:
            raise KeyError(name)
    return np.concatenate(units, axis=0)


def grid_pos_embed_np(rows):
    rr, cc = np.meshgrid(np.arange(rows, dtype=np.float32), np.arange(64, dtype=np.float32), indexing="ij")
    quarter = D // 4
    omega = (1.0 / (10000.0 ** (np.arange(quarter, dtype=np.float32) / np.float32(quarter)))).astype(np.float32)
    er = rr.reshape(-1, 1) * omega
    ec = cc.reshape(-1, 1) * omega
    return np.concatenate([np.sin(er), np.cos(er), np.sin(ec), np.cos(ec)], axis=-1).astype(np.float32)


def to_fm3(x):
    return np.ascontiguousarray(np.asarray(x, np.float32).T.reshape(8, 128, 1024).transpose(1, 0, 2))


def from_fm3(y):
    return np.ascontiguousarray(y.transpose(1, 0, 2).reshape(1024, 1024).T)


def mix_mats(prompt):
    S = 256 if prompt else 1024
    nseq = T // S
    mats = []
    t = np.arange(S)
    for win in (2, 4, 8, 16):
        lo = np.clip(t - win // 2, 0, S)
        hi = np.clip(t + win - win // 2, 0, S)
        A = np.zeros((S, S), np.float64)
        for tt in range(S):
            A[tt, lo[tt]:hi[tt]] = 1.0 / (hi[tt] - lo[tt])
        A -= np.eye(S)
        B = A.T
        full = np.zeros((T, T), np.float64)
        for q in range(nseq):
            full[q * S:(q + 1) * S, q * S:(q + 1) * S] = B
        mats.append(full)
    ang = 2.0 * np.pi * np.outer(t, t) / S
    nrm = 1.0 / np.sqrt(S * 256.0)
    for M in (np.cos(ang) * nrm, -np.sin(ang) * nrm):
        full = np.zeros((T, T), np.float64)
        for q in range(nseq):
            full[q * S:(q + 1) * S, q * S:(q + 1) * S] = M
        mats.append(full)
    out = np.stack([m.reshape(8, 128, T).transpose(1, 0, 2).reshape(128, UNITC) for m in mats], axis=0)
    return out.astype(ml_dtypes.bfloat16)


def const_bf():
    c = np.zeros((128, NCB), np.float32)
    c[:, CB["ident"]:CB["ident"] + 128] = np.eye(128)
    c[:, CB["ones"]:CB["ones"] + 128] = 1.0
    s = np.arange(128)[:, None]
    t = np.arange(128)[None, :]
    c[:, CB["maskF"]:CB["maskF"] + 128] = (s <= t)
    c[:, CB["maskB"]:CB["maskB"] + 128] = (s >= t)
    cc = np.arange(256)
    ang = 2.0 * np.pi * np.outer(cc, cc) / 256.0
    cw = np.cos(ang)
    sw = np.sin(ang)
    both = np.concatenate([cw, sw], axis=1)
    c[:, CB["cwsw"]:CB["cwsw"] + 1024] = both.reshape(2, 128, 512).transpose(1, 0, 2).reshape(128, 1024)
    return c.astype(ml_dtypes.bfloat16)


def const_f32():
    c = np.zeros((128, NCF), np.float32)
    c[:, CF["identf"]:CF["identf"] + 128] = np.eye(128)
    t = np.arange(1024)
    rs1 = np.ones((128, 1024), np.float32)
    rsm = np.zeros((128, 1024), np.float32)
    rs1[0:32, t % 256 == 0] = 0.0
    rsm[0:32, t % 256 == 0] = -1e30
    rs1[32:64, t % 256 == 0] = 0.0
    rsm[32:64, t % 256 == 0] = -1e30
    c[:, CF["rs1f"]:CF["rs1f"] + 1024] = rs1
    c[:, CF["rsmf"]:CF["rsmf"] + 1024] = rsm
    dm = np.zeros((128, 16), np.float32)
    for d in range(2):
        for h in range(8):
            dm[32 * d + h, d * 8 + h] = 1.0
    c[:, CF["dmask"]:CF["dmask"] + 16] = dm
    c[:, CF["ones128"]:CF["ones128"] + 128] = 1.0
    return c


def build_program(cfg):
    nc = bass.Bass("TRN2", target_bir_lowering=False)
    nc.dge_precook = False
    plan = unit_plan()
    NU = len(plan)
    uidx = {name: k for k, (name, _) in enumerate(plan)}

    def dram(name, shape, dt, kind="ExternalInput"):
        return nc.dram_tensor(name, shape, dt, kind=kind).ap()

    xT_d = dram("xT", [128, 8, 1024], F32)
    posT_d = dram("posT", [128, 8, 1024], F32)
    condb_d = dram("condb", [128, 1024], F32)
    vecs_d = dram("vecs", [128, NV], F32)
    wmodT_d = dram("wmodT", [4 * 6144, 1024], F32)
    wall_d = dram("wall", [NU * 128, UNITC], F32)
    mixm_d = dram("mixm", [6 * 128, UNITC], BF16)
    cbf_d = dram("cbf", [128, NCB], BF16)
    cf_d = dram("cf", [128, NCF], F32)
    st0_d = dram("st0", [64, 16, 130], F32)
    ngb_d = dram("ngb", [128, 1024], F32)
    yT_d = dram("yT", [128, 8, 1024], F32, kind="ExternalOutput")
    stE_d = dram("stE", [64, 64, 130], F32, kind="ExternalOutput")
    stm_d = dram("stm", [64, 4], F32, kind="ExternalOutput")

    P = Prog(nc, same_engine_sync=cfg.get("same_engine_sync", True))
    xa = P.sb("xa", [128, 8, 1024], F32)
    hT = P.sb("hT", [128, 8, 1024], BF16)
    SCR_B = 90 * 1024
    scr = P.sb("scr", [128, SCR_B // 2], BF16)
    ring = [P.sb("ring%d" % i, [128, UNITC], BF16) for i in range(RING)]
    wm = [P.sb("wm%d" % i, [128, 1024], F32) for i in range(2)]
    vecs = P.sb("vecs", [128, NV], F32)
    cbf = P.sb("cbf", [128, NCB], BF16)
    modr = P.sb("modr", [128, 4 * 48], F32)
    dv = P.sb("dv", [128, 4 * 64], F32)
    junk = P.sb("junk", [128, 1024], BF16)
    junk2 = P.sb("junk2", [128, 128], BF16)
    sb_t = P.sb("sb_t", [128, 1024], F32)
    banks = [P.ps("pb%d" % i, [128, 512], F32) for i in range(8)]

    def sview(byte_off, nbytes, dtype, pattern=None, **kw):
        v = scr[:, byte_off // 2:(byte_off + nbytes) // 2]
        if dtype != BF16:
            v = v.bitcast(dtype)
        if pattern:
            v = v.rearrange(pattern, **kw)
        return v

    def vcol(name, j=0, n=1):
        o = VC[name] + j
        return vecs[:, o:o + n]

    P.dma("sync", lambda e: e.dma_start(out=vecs[:], in_=vecs_d[:, :]), writes=["vecs"])
    P.dma("sync", lambda e: e.dma_start(out=sb_t[:], in_=condb_d[:, :]), writes=["sb_t"])
    P.dma("sync", lambda e: e.dma_start(out=cbf[:], in_=cbf_d[:, :]), writes=["cbf"])
    P.op("scalar", lambda e: e.activation(out=sb_t[:], in_=sb_t[:], func=AF.Silu), reads=["sb_t"], writes=["sb_t"])

    ustate = {"next": 0}

    def want_unit(u, la=RING - 1):
        while ustate["next"] < NU and ustate["next"] <= u + la:
            v = ustate["next"]
            slot = v % RING
            ncols = plan[v][1]
            P.dma("gpsimd", lambda e, v=v, slot=slot, ncols=ncols: e.dma_start(
                out=ring[slot][:, 0:ncols], in_=wall_d[v * 128:(v + 1) * 128, 0:ncols], max_dma_last_dim=4096),
                writes=[("W", slot)])
            ustate["next"] += 1
        return ring[u % RING], ("W", u % RING)

    wmstate = {"n": 0}

    def mod_step(i, c):
        k = wmstate["n"]
        wmstate["n"] += 1
        s = k % 2
        row = (i * 48 + c) * 128
        P.dma("sync", lambda e: e.dma_start(out=wm[s][:], in_=wmodT_d[row:row + 128, :]), writes=[("wm", s)])
        col = i * 48 + c
        P.op("vector", lambda e: e.scalar_tensor_tensor(out=junk[:], in0=wm[s][:], scalar=1.0, in1=sb_t[:],
                                                        op0=ALU.mult, op1=ALU.mult, accum_out=modr[:, col:col + 1]),
             reads=[("wm", s), "sb_t"], writes=[("modr", i, c), "junk"])

    def mod_steps(i):
        for c in range(48):
            yield (i, c)

    def mod_finish_a(i):
        o = i * 48
        b = i * 64
        P.op("vector", lambda e: e.tensor_tensor(out=modr[:, o:o + 16], in0=modr[:, o:o + 16],
                                                 in1=vcol("bmod", o, 16), op=ALU.add),
             reads=[("modr", i, j) for j in range(16)] + ["vecs"], writes=[("modA", i)])
        P.op("vector", lambda e: e.tensor_scalar(out=dv[:, b:b + 8], in0=modr[:, o + 8:o + 16], scalar1=1.0,
                                                 scalar2=1.0 / ALPHA, op0=ALU.add, op1=ALU.mult),
             reads=[("modA", i)], writes=[("dvA", i)])

    def mod_finish_b(i):
        last = (i == DEPTH - 1)
        o = i * 48
        P.op("vector", lambda e: e.tensor_tensor(out=modr[:, o + 16:o + 48], in0=modr[:, o + 16:o + 48],
                                                 in1=vcol("bmod", o + 16, 32), op=ALU.add),
             reads=[("modr", i, j) for j in range(16, 48)] + ["vecs"], writes=[("mod", i)])
        b = i * 64
        P.op("vector", lambda e: e.tensor_scalar(out=dv[:, b + 8:b + 16], in0=modr[:, o + 32:o + 40], scalar1=1.0,
                                                 scalar2=1.0 / ALPHA, op0=ALU.add, op1=ALU.mult),
             reads=[("mod", i)], writes=[("dv", i)])
        for j in range(2):
            sc = 1.0 if (last and j == 1) else ALPHA
            lo = VC["lng"] + (i * 2 + j) * 8
            lb = VC["lnb"] + (i * 2 + j) * 8
            P.op("vector", lambda e, lo=lo, j=j, sc=sc: e.tensor_scalar(out=dv[:, b + 16 + 16 * j:b + 24 + 16 * j], in0=vecs[:, lo:lo + 8],
                                                                       scalar1=sc, scalar2=None, op0=ALU.mult),
                 reads=["vecs", ("dv", i)], writes=[("dv", i)])
            P.op("vector", lambda e, lb=lb, j=j, sc=sc: e.tensor_scalar(out=dv[:, b + 24 + 16 * j:b + 32 + 16 * j], in0=vecs[:, lb:lb + 8],
                                                                       scalar1=sc, scalar2=None, op0=ALU.mult),
                 reads=["vecs", ("dv", i)], writes=[("dv", i)])
        P.op("vector", lambda e: e.tensor_copy(out=junk[:, 0:1], in_=junk[:, 0:1]), reads=[("mod", i), ("dv", i)], writes=["modall", "junk"])

    pend = {"it": iter(()), "layer": None, "done": True}

    def pump(n):
        for _ in range(n):
            st = next(pend["it"], None)
            if st is None:
                break
            mod_step(*st)

    def finish_pending():
        if pend["done"]:
            return
        pump(1000)
        mod_finish_b(pend["layer"])
        pend["done"] = True

    def g1(i, c):
        return modr[:, i * 48 + 16 + c:i * 48 + 17 + c]

    def g2(i, c):
        return modr[:, i * 48 + 40 + c:i * 48 + 41 + c]

    def sh(i, j, c):
        o = i * 48 + (0 if j == 0 else 24) + c
        return modr[:, o:o + 1]

    def mm(i, j, c):
        o = i * 64 + 8 * j + c
        return dv[:, o:o + 1]

    def Ga(i, j, c):
        o = i * 64 + 16 + 16 * j + c
        return dv[:, o:o + 1]

    def Ba(i, j, c):
        o = i * 64 + 24 + 16 * j + c
        return dv[:, o:o + 1]

    bank_rr = {"n": 0}

    def next_bank(pool=(0, 1, 2, 3)):
        b = pool[bank_rr["n"] % len(pool)]
        bank_rr["n"] += 1
        return b

    def modulate(i, j, halves=(0, 1)):
        for hf in halves:
            for c in range(8):
                sl = slice(hf * 512, (hf + 1) * 512)
                P.op("scalar", lambda e, c=c, sl=sl: e.activation(out=hT[:, c, sl], in_=xa[:, c, sl], func=AF.Identity,
                                                                  scale=mm(i, j, c), bias=sh(i, j, c)),
                     reads=[("xa", c, hf)] + ([("dvA", i), ("modA", i)] if j == 0 else [("dv", i), ("mod", i)]), writes=[("H", c, hf)])

    OFF_AT = 0
    OFF_ZB = 64 * 1024
    OFF_ZSQ = 72 * 1024
    OFF_ST = 80 * 1024
    OFF_RT = 86 * 1024
    zb = sview(OFF_ZB, 8192, BF16, "p (c t) -> p c t", c=8)
    zsq = sview(OFF_ZSQ, 8192, BF16, "p (c t) -> p c t", c=8)
    mu = sview(OFF_ST, 2048, F32)
    musq = sview(OFF_ST + 2048, 2048, F32)
    var = sview(OFF_ST + 4096, 2048, F32)
    rtmp = [sview(OFF_RT + k * 2048, 2048, F32) for k in range(2)]
    ones_bf = cbf[:, CB["ones"]:CB["ones"] + 128]
    ident_bf = cbf[:, CB["ident"]:CB["ident"] + 128]

    def ln_phases(i, j, hf, do_h=True, nexti=None, nextj=None):
        sl = slice(hf * 512, (hf + 1) * 512)
        for c in range(8):
            P.op("scalar", lambda e, c=c, sl=sl: e.activation(out=zb[:, c, :], in_=xa[:, c, sl], func=AF.Copy),
                 reads=[("xa", c, hf)], writes=[("S", "zb", c)])
            P.op("scalar", lambda e, c=c, sl=sl: e.activation(out=zsq[:, c, :], in_=xa[:, c, sl], func=AF.Square),
                 reads=[("xa", c, hf)], writes=[("S", "zsq", c)])
        yield 1
        for c in range(8):
            P.op("tensor", lambda e, c=c: e.matmul(banks[4][:], lhsT=ones_bf, rhs=zb[:, c, :], start=(c == 0), stop=(c == 7)),
                 reads=[("S", "zb", c), "cbf"], writes=[("ps", 4)], chain=True)
        for c in range(8):
            P.op("tensor", lambda e, c=c: e.matmul(banks[5][:], lhsT=ones_bf, rhs=zsq[:, c, :], start=(c == 0), stop=(c == 7)),
                 reads=[("S", "zsq", c), "cbf"], writes=[("ps", 5)], chain=True)
        P.op("scalar", lambda e: e.mul(out=mu, in_=banks[4][:], mul=1.0 / D), reads=[("ps", 4)], writes=[("S", "mu")])
        P.op("vector", lambda e: e.tensor_tensor(out=musq, in0=mu, in1=mu, op=ALU.mult), reads=[("S", "mu")], writes=[("S", "musq")])
        P.op("vector", lambda e: e.scalar_tensor_tensor(out=var, in0=banks[5][:], scalar=1.0 / D, in1=musq,
                                                        op0=ALU.mult, op1=ALU.subtract),
             reads=[("ps", 5), ("S", "musq")], writes=[("S", "var")])
        P.op("vector", lambda e: e.tensor_scalar(out=var, in0=var, scalar1=LN_EPS, scalar2=None, op0=ALU.add),
             reads=[("S", "var")], writes=[("S", "var")])
        P.op("scalar", lambda e: e.activation(out=var, in_=var, func=AF.Sqrt), reads=[("S", "var")], writes=[("S", "var")])
        P.op("vector", lambda e: e.reciprocal(out=var, in_=var), reads=[("S", "var")], writes=[("S", "var")])
        yield 2
        for c in range(8):
            P.op("vector", lambda e, c=c, sl=sl: e.tensor_tensor(out=xa[:, c, sl], in0=xa[:, c, sl], in1=mu, op=ALU.subtract),
                 reads=[("xa", c, hf), ("S", "mu")], writes=[("xa", c, hf)])
            P.op("vector", lambda e, c=c, sl=sl: e.tensor_tensor(out=xa[:, c, sl], in0=xa[:, c, sl], in1=var, op=ALU.mult),
                 reads=[("xa", c, hf), ("S", "var")], writes=[("xa", c, hf)])
            P.op("scalar", lambda e, c=c, sl=sl: e.activation(out=xa[:, c, sl], in_=xa[:, c, sl], func=AF.Identity,
                                                              scale=Ga(i, j, c), bias=Ba(i, j, c)),
                 reads=[("xa", c, hf), ("dv", i)], writes=[("xa", c, hf)])
        if do_h:
            modulate(nexti, nextj, halves=(hf,))
        yield 3

    def layer_norm(i, j, do_h=True, nexti=None, nextj=None):
        for hf in range(2):
            for _ in ln_phases(i, j, hf, do_h, nexti, nextj):
                pass

    def epilogue(ps_bank, gvec, c, hf, extra=()):
        sl = slice(hf * 512, (hf + 1) * 512)
        P.op("vector", lambda e: e.scalar_tensor_tensor(out=xa[:, c, sl], in0=banks[ps_bank][:], scalar=gvec, in1=xa[:, c, sl],
                                                        op0=ALU.mult, op1=ALU.add),
             reads=[("ps", ps_bank), ("xa", c, hf), "modall"] + list(extra), writes=[("xa", c, hf)])

    def out_proj(i, unit_name, srcT, src_res, gfun):
        u = uidx[unit_name] if not isinstance(unit_name, int) else unit_name
        rt, rres = want_unit(u)
        w = rt[:, :].rearrange("p (k n) -> p k n", k=8)
        for dc in range(8):
            for hf in range(2):
                b = next_bank()
                for kc in range(8):
                    P.op("tensor", lambda e, b=b, kc=kc, dc=dc, hf=hf: e.matmul(
                        banks[b][:], lhsT=w[:, kc, dc * 128:(dc + 1) * 128], rhs=srcT[:, kc, hf * 512:(hf + 1) * 512],
                        start=(kc == 0), stop=(kc == 7)),
                        reads=[rres] + (src_res(kc, hf) if isinstance(src_res(kc, hf), list) else [src_res(kc, hf)]), writes=[("ps", b)], chain=True)
                epilogue(b, gfun(dc), dc, hf)

    aT = sview(OFF_AT, 65536, BF16, "p (f t) -> p f t", f=32)

    def mlp(i, steps, nxt, g1=iter(())):
        cnt = {"g": 0}

        def w1_group(w, rres, f, fl, hf):
            b = next_bank()
            sl = slice(hf * 512, (hf + 1) * 512)
            for kc in range(8):
                P.op("tensor", lambda e, b=b, kc=kc, fl=fl, sl=sl, w=w: e.matmul(
                    banks[b][:], lhsT=w[:, kc, fl * 128:(fl + 1) * 128], rhs=hT[:, kc, sl],
                    start=(kc == 0), stop=(kc == 7)),
                    reads=[rres, ("H", kc, hf)], writes=[("ps", b)], chain=True)
            r = cnt["g"] % 2
            P.op("scalar", lambda e, b=b, r=r: e.activation(out=rtmp[r], in_=banks[b][:], func=AF.Relu),
                 reads=[("ps", b)], writes=[("S", "rt", r)])
            P.op("scalar", lambda e, r=r, f=f, sl=sl: e.activation(out=aT[:, f, sl], in_=rtmp[r], func=AF.Square),
                 reads=[("S", "rt", r)], writes=[("S", "aT", f, hf)])
            if cnt["g"] % 4 != 3:
                st = next(steps, None)
                if st is not None:
                    mod_step(*st)
            cnt["g"] += 1

        ws = []
        for blk in range(2):
            rt, rres = want_unit(uidx["w1_%d_%d" % (i, blk)], la=(RING - 1 if blk == 0 else 1))
            ws.append((rt[:, :].rearrange("p (k n) -> p k n", k=8), rres))
        for hf in range(2):
            for blk in range(2):
                w, rres = ws[blk]
                for fl in range(8):
                    f = blk * 8 + fl
                    if hf == 0 and f in (1, 4, 7):
                        next(g1, None)
                    w1_group(w, rres, f, fl, hf)
            if hf == 0:
                for _ in g1:
                    pass
        for blk in range(2, 4):
            rt, rres = want_unit(uidx["w1_%d_%d" % (i, blk)])
            w = rt[:, :].rearrange("p (k n) -> p k n", k=8)
            for fl in range(8):
                for hf in range(2):
                    w1_group(w, rres, blk * 8 + fl, fl, hf)
        do_h = nxt is not None
        g = iter(())
        ngrp = 0
        for hf in range(2):
            sl = slice(hf * 512, (hf + 1) * 512)
            for s in range(4):
                rt, rres = want_unit(uidx[("w2_%d_%d" if hf == 0 else "w2b_%d_%d") % (i, s)])
                w = rt[:, :].rearrange("p (f n) -> p f n", f=32)
                for dl in range(2):
                    dc = 2 * s + dl
                    b = next_bank()
                    for f in range(32):
                        P.op("tensor", lambda e, b=b, f=f, dl=dl, sl=sl, w=w: e.matmul(
                            banks[b][:], lhsT=w[:, f, dl * 128:(dl + 1) * 128], rhs=aT[:, f, sl],
                            start=(f == 0), stop=(f == 31)),
                            reads=[rres, ("S", "aT", f, hf)], writes=[("ps", b)], chain=True)
                    epilogue(b, g2(i, dc), dc, hf)
                    for _ in range(3):
                        st = next(steps, None)
                        if st is not None:
                            mod_step(*st)
                    if hf == 1:
                        next(g, None)
            if hf == 0:
                if nxt is not None:
                    mod_finish_a(nxt)
                g = ln_phases(i, 1, 0, do_h, nxt, 0)
        for _ in g:
            pass
        for st in steps:
            mod_step(*st)
        if nxt is not None:
            mod_finish_b(nxt)
        for _ in ln_phases(i, 1, 1, do_h, nxt, 0):
            pass

    def mixer_conv(i):
        OFF_U, OFF_CU, OFF_BC, OFF_C0 = 0, 16384, 32768, 49152
        uT = sview(OFF_U, 16384, BF16, "p (c t) -> p c t", c=8)
        cuT = sview(OFF_CU, 16384, BF16, "p (c t) -> p c t", c=8)
        bcT = sview(OFF_BC, 16384, BF16, "p (c t) -> p c t", c=8)
        c0 = [sview(OFF_C0 + k * 4096, 4096, F32) for k in range(2)]
        for which, uname in ((0, "sc_u"), (1, "sc_cg")):
            rt, rres = want_unit(uidx[uname])
            w = rt[:, :].rearrange("p (k n) -> p k n", k=8)
            for nc_ in range(8):
                for hf in range(2):
                    sl = slice(hf * 512, (hf + 1) * 512)
                    b = next_bank()
                    for kc in range(8):
                        P.op("tensor", lambda e, b=b, kc=kc, nc_=nc_, sl=sl, w=w: e.matmul(
                            banks[b][:], lhsT=w[:, kc, nc_ * 128:(nc_ + 1) * 128], rhs=hT[:, kc, sl],
                            start=(kc == 0), stop=(kc == 7)),
                            reads=[rres, ("H", kc, hf)], writes=[("ps", b)], chain=True)
                    if which == 0:
                        P.op("scalar", lambda e, b=b, nc_=nc_, sl=sl: e.activation(out=uT[:, nc_, sl], in_=banks[b][:], func=AF.Copy),
                             reads=[("ps", b)], writes=[("S", "u", nc_, hf)])
                    else:
                        P.op("vector", lambda e, b=b, nc_=nc_, sl=sl: e.tensor_tensor(out=cuT[:, nc_, sl], in0=banks[b][:], in1=uT[:, nc_, sl], op=ALU.mult),
                             reads=[("ps", b), ("S", "u", nc_, hf)], writes=[("S", "cu", nc_, hf)])
        rt, rres = want_unit(uidx["sc_bg"])
        w = rt[:, :].rearrange("p (k n) -> p k n", k=8)
        for nc_ in range(8):
            cc = c0[nc_ % 2]
            cres = ("S", "c0", nc_ % 2)
            cw = lambda k, nc_=nc_: vecs[:, VC["convw"] + k * 8 + nc_:VC["convw"] + k * 8 + nc_ + 1]
            rd = [("S", "cu", nc_, 0), ("S", "cu", nc_, 1), "vecs"]
            P.op("vector", lambda e, nc_=nc_, cc=cc, cw=cw: e.tensor_scalar(out=cc, in0=cuT[:, nc_, :], scalar1=cw(1), scalar2=None, op0=ALU.mult),
                 reads=rd, writes=[cres])
            P.op("vector", lambda e, nc_=nc_, cc=cc, cw=cw: e.scalar_tensor_tensor(out=cc[:, 1:1024], in0=cuT[:, nc_, 0:1023], scalar=cw(0), in1=cc[:, 1:1024],
                                                                                  op0=ALU.mult, op1=ALU.add),
                 reads=rd + [cres], writes=[cres])
            P.op("vector", lambda e, nc_=nc_, cc=cc, cw=cw: e.scalar_tensor_tensor(out=cc[:, 0:1023], in0=cuT[:, nc_, 1:1024], scalar=cw(2), in1=cc[:, 0:1023],
                                                                                  op0=ALU.mult, op1=ALU.add),
                 reads=rd + [cres], writes=[cres])
            ccv = cc.rearrange("p (s t) -> p s t", s=4)
            cuv = cuT[:, nc_, :].rearrange("p (s t) -> p s t", s=4)
            nb0 = sview(OFF_C0 + 8192 + (nc_ % 2) * 64, 32, F32, "p (s t) -> p s t", s=4)
            P.op("vector", lambda e, cuv=cuv, nb0=nb0, cw=cw: e.tensor_scalar(out=nb0[:, 0:3, 0:1], in0=cuv[:, 0:3, 255:256], scalar1=cw(0), scalar2=vcol("bflag"),
                                                                             op0=ALU.mult, op1=ALU.mult),
                 reads=rd, writes=[("S", "nb", nc_ % 2)])
            P.op("vector", lambda e, cuv=cuv, nb0=nb0, cw=cw: e.tensor_scalar(out=nb0[:, 0:3, 1:2], in0=cuv[:, 1:4, 0:1], scalar1=cw(2), scalar2=vcol("bflag"),
                                                                             op0=ALU.mult, op1=ALU.mult),
                 reads=rd + [("S", "nb", nc_ % 2)], writes=[("S", "nb", nc_ % 2)])
            P.op("vector", lambda e, ccv=ccv, nb0=nb0: e.tensor_tensor(out=ccv[:, 1:4, 0:1], in0=ccv[:, 1:4, 0:1], in1=nb0[:, 0:3, 0:1], op=ALU.subtract),
                 reads=[cres, ("S", "nb", nc_ % 2)], writes=[cres])
            P.op("vector", lambda e, ccv=ccv, nb0=nb0: e.tensor_tensor(out=ccv[:, 0:3, 255:256], in0=ccv[:, 0:3, 255:256], in1=nb0[:, 0:3, 1:2], op=ALU.subtract),
                 reads=[cres, ("S", "nb", nc_ % 2)], writes=[cres])
            for hf in range(2):
                sl = slice(hf * 512, (hf + 1) * 512)
                b = next_bank()
                for kc in range(8):
                    P.op("tensor", lambda e, b=b, kc=kc, nc_=nc_, sl=sl, w=w: e.matmul(
                        banks[b][:], lhsT=w[:, kc, nc_ * 128:(nc_ + 1) * 128], rhs=hT[:, kc, sl],
                        start=(kc == 0), stop=(kc == 7)),
                        reads=[rres, ("H", kc, hf)], writes=[("ps", b)], chain=True)
                P.op("vector", lambda e, b=b, nc_=nc_, sl=sl, cc=cc: e.tensor_tensor(out=bcT[:, nc_, sl], in0=banks[b][:], in1=cc[:, sl], op=ALU.mult),
                     reads=[("ps", b), cres], writes=[("S", "bc", nc_, hf)])
        out_proj(i, "sc_out", bcT, lambda kc, hf: ("S", "bc", kc, hf), lambda dc: g1(i, dc))

    def load_mix(k, slot_hint):
        raise NotImplementedError

    def mixer_pool(i):
        OFF_HT, OFF_PT, OFF_MM = 0, 16384, 32768
        htok = sview(OFF_HT, 16384, BF16, "p (t d) -> p t d", t=8)
        pT = sview(OFF_PT, 16384, BF16, "p (c t) -> p c t", c=8)
        mbuf = [sview(OFF_MM + k * 16384, 16384, BF16, "p (s t) -> p s t", s=8) for k in range(2)]
        gsb = sview(OFF_MM + 32768, 64, F32)
        P.op("vector", lambda e: e.tensor_tensor(out=gsb[:, 0:8], in0=modr[:, i * 48 + 16:i * 48 + 24], in1=vcol("pls", 0, 8), op=ALU.mult),
             reads=["modall", "vecs"], writes=[("S", "gsb")])
        P.op("vector", lambda e: e.tensor_tensor(out=gsb[:, 8:16], in0=gsb[:, 0:8], in1=vcol("plb", 0, 8), op=ALU.mult),
             reads=[("S", "gsb"), "vecs"], writes=[("S", "gsb")])
        for tt in range(8):
            b = next_bank((6, 7))
            pv = banks[b][:].bitcast(BF16)
            for c in range(8):
                P.op("tensor", lambda e, pv=pv, c=c, tt=tt: e.transpose(out=pv[:, c * 128:(c + 1) * 128], in_=hT[:, c, tt * 128:(tt + 1) * 128], identity=ident_bf),
                     reads=[("H", c, tt // 4), "cbf"], writes=[("ps", b)])
            P.op("scalar", lambda e, pv=pv, tt=tt: e.activation(out=htok[:, tt, :], in_=pv, func=AF.Copy),
                 reads=[("ps", b)], writes=[("S", "htok", tt)])
        for c in range(8):
            for hf in range(2):
                sl = slice(hf * 512, (hf + 1) * 512)
                P.op("vector", lambda e, c=c, sl=sl: e.tensor_scalar(out=xa[:, c, sl], in0=xa[:, c, sl], scalar1=gsb[:, 8 + c:9 + c], scalar2=None, op0=ALU.add),
                     reads=[("xa", c, hf), ("S", "gsb")], writes=[("xa", c, hf)])
        for g in range(4):
            mb = mbuf[g % 2]
            P.dma("sync", lambda e, g=g, mb=mb: e.dma_start(out=mb, in_=mixm_d[g * 128:(g + 1) * 128, :].rearrange("p (s t) -> p s t", s=8)),
                  writes=[("S", "mb", g % 2)])
            for cl in range(2):
                c = 2 * g + cl
                for hf in range(2):
                    sl = slice(hf * 512, (hf + 1) * 512)
                    b = next_bank()
                    for sc in range(8):
                        P.op("tensor", lambda e, b=b, sc=sc, c=c, sl=sl, mb=mb: e.matmul(
                            banks[b][:], lhsT=htok[:, sc, c * 128:(c + 1) * 128], rhs=mb[:, sc, sl], start=(sc == 0), stop=(sc == 7)),
                            reads=[("S", "htok", sc), ("S", "mb", g % 2)], writes=[("ps", b)], chain=True)
                    P.op("scalar", lambda e, b=b, c=c, sl=sl: e.activation(out=pT[:, c, sl], in_=banks[b][:], func=AF.Copy),
                         reads=[("ps", b)], writes=[("S", "pT", c, hf)])
        rt, rres = want_unit(uidx["pl_w"])
        w = rt[:, 0:2048].rearrange("p (g c n) -> p g c n", g=4, c=2)
        for g in range(4):
            for dl in range(2):
                dc = 2 * g + dl
                for hf in range(2):
                    sl = slice(hf * 512, (hf + 1) * 512)
                    b = next_bank()
                    for cl in range(2):
                        P.op("tensor", lambda e, b=b, g=g, cl=cl, dl=dl, sl=sl: e.matmul(
                            banks[b][:], lhsT=w[:, g, cl, dl * 128:(dl + 1) * 128], rhs=pT[:, 2 * g + cl, sl], start=(cl == 0), stop=(cl == 1)),
                            reads=[rres, ("S", "pT", 2 * g + cl, hf)], writes=[("ps", b)], chain=True)
                    epilogue(b, gsb[:, dc:dc + 1], dc, hf, extra=[("S", "gsb")])

    def mixer_fourier(i):
        OFF_P1, OFF_P2, OFF_FT, OFF_M = 0, 16384, 32768, 49152
        P1 = sview(OFF_P1, 16384, BF16, "p (t n) -> p t n", t=8)
        P2 = sview(OFF_P2, 16384, BF16, "p (t n) -> p t n", t=8)
        FT = sview(OFF_FT, 16384, BF16, "p (c t) -> p c t", c=8)
        mb = [sview(OFF_M + k * 16384, 16384, BF16, "p (s t) -> p s t", s=8) for k in range(2)]
        gb = sview(OFF_M + 32768, 32, F32)
        cwsw = cbf[:, CB["cwsw"]:CB["cwsw"] + 1024].rearrange("p (c n) -> p c n", c=2)
        for k in range(2):
            P.dma("sync", lambda e, k=k: e.dma_start(out=mb[k], in_=mixm_d[(4 + k) * 128:(5 + k) * 128, :].rearrange("p (s t) -> p s t", s=8)),
                  writes=[("S", "fm", k)])
        P.op("vector", lambda e: e.tensor_tensor(out=gb[:, 0:8], in0=modr[:, i * 48 + 16:i * 48 + 24], in1=vcol("ftb", 0, 8), op=ALU.mult),
             reads=["modall", "vecs"], writes=[("S", "gb")])
        for c in range(8):
            for hf in range(2):
                sl = slice(hf * 512, (hf + 1) * 512)
                P.op("vector", lambda e, c=c, sl=sl: e.tensor_scalar(out=xa[:, c, sl], in0=xa[:, c, sl], scalar1=gb[:, c:c + 1], scalar2=None, op0=ALU.add),
                     reads=[("xa", c, hf), ("S", "gb")], writes=[("xa", c, hf)])
        for tt in range(8):
            for g in range(4):
                b = next_bank()
                for cl in range(2):
                    P.op("tensor", lambda e, b=b, g=g, cl=cl, tt=tt: e.matmul(
                        banks[b][:], lhsT=hT[:, 2 * g + cl, tt * 128:(tt + 1) * 128], rhs=cwsw[:, cl, :], start=(cl == 0), stop=(cl == 1)),
                        reads=[("H", 2 * g + cl, tt // 4), "cbf"], writes=[("ps", b)], chain=True)
                P.op("scalar", lambda e, b=b, g=g, tt=tt: e.activation(out=P1[:, tt, g * 256:(g + 1) * 256], in_=banks[b][:, 0:256], func=AF.Copy),
                     reads=[("ps", b)], writes=[("S", "P1", tt, g)])
                P.op("vector", lambda e, b=b, g=g, tt=tt: e.tensor_copy(out=P2[:, tt, g * 256:(g + 1) * 256], in_=banks[b][:, 256:512]),
                     reads=[("ps", b)], writes=[("S", "P2", tt, g), ("ps", b)])
        for c in range(8):
            g = c // 2
            for hf in range(2):
                sl = slice(hf * 512, (hf + 1) * 512)
                b = next_bank()
                n = 0
                for k, Pk, nm in ((0, P1, "P1"), (1, P2, "P2")):
                    for tt in range(8):
                        P.op("tensor", lambda e, b=b, k=k, Pk=Pk, tt=tt, c=c, sl=sl, n=n: e.matmul(
                            banks[b][:], lhsT=Pk[:, tt, c * 128:(c + 1) * 128], rhs=mb[k][:, tt, sl], start=(n == 0), stop=(n == 15)),
                            reads=[("S", nm, tt, g), ("S", "fm", k)], writes=[("ps", b)], chain=True)
                        n += 1
                P.op("scalar", lambda e, b=b, c=c, sl=sl: e.activation(out=FT[:, c, sl], in_=banks[b][:], func=AF.Copy),
                     reads=[("ps", b)], writes=[("S", "FT", c, hf)])
        out_proj(i, "ft_out", FT, lambda kc, hf: ("S", "FT", kc, hf), lambda dc: g1(i, dc))

    def mixer_mlstm(i):
        OV, OAK, OCF, OAB, OSM, OCST, OST, OEF = 0, 16640, 22784, 23872, 24128, 25152, 27232, 28768
        OHF, OSEG, OHS, OTMP, OHN = 29824, 46208, 62592, 70784, 74880
        Vx = sview(OV, 16640, BF16, "p (t h n) -> p t h n", t=8, h=8)
        akT = sview(OAK, 6144, F32, "p (t q r) -> p t q r", t=8, q=3)
        identf = sview(OCF, 512, F32)
        dmask = sview(OCF + 512, 64, F32)
        ones128 = sview(OCF + 576, 512, F32)
        aendb = sview(OAB, 256, F32)
        sm = sview(OSM, 1024, F32)
        mst, nmst, mend, aend = sm[:, 0:4], sm[:, 4:8], sm[:, 8:12], sm[:, 12:16]
        Xd = sm[:, 16:80]
        s1, s2, mean_, rstd_ = sm[:, 80:88], sm[:, 88:96], sm[:, 96:104], sm[:, 104:112]
        dn = [sm[:, 112 + 2 * k:114 + 2 * k] for k in range(2)]
        Cst = sview(OCST, 2080, BF16, "p (d q n) -> p d q n", d=2, q=4)
        St = [sview(OST + k * 768, 768, BF16, "p (b t) -> p b t", b=3) for k in range(2)]
        Ef = [sview(OEF + k * 520, 520, F32) for k in range(2)]
        hf = sview(OHF, 16384, BF16, "p (t n) -> p t n", t=8)
        qkt = [sview(OSEG + k * 2048, 2048, BF16) for k in range(4)]
        qTs = [sview(OSEG + 8192 + k * 2048, 2048, BF16, "p (q t) -> p q t", q=4) for k in range(2)]
        kTs = [sview(OSEG + 12288 + k * 2048, 2048, BF16, "p (q t) -> p q t", q=4) for k in range(2)]
        hs = sview(OHS, 8192, F32, "p (j n) -> p j n", j=2)
        sg = sview(OTMP, 2048, F32)
        hgt = sview(OTMP + 2048, 2048, BF16)
        hnT = sview(OHN, 16384, BF16, "p (c t) -> p c t", c=8)
        Cf32 = sview(OHN, 4160, F32, "p (d q n) -> p d q n", d=2, q=4)
        gt = [sview(OHF + k * 4096, 4096, F32) for k in range(10)]
        gi, lf, bb, uu, cm, al, ka, ep, rs1, rsm = gt
        R = slice(0, 64)
        maskF = cbf[:, CB["maskF"]:CB["maskF"] + 128]
        maskB = cbf[:, CB["maskB"]:CB["maskB"] + 128]
        S_ = lambda *k: ("S",) + k

        P.dma("sync", lambda e: e.dma_start(out=identf, in_=cf_d[:, CF["identf"]:CF["identf"] + 128]), writes=[S_("identf")])
        P.dma("sync", lambda e: e.dma_start(out=dmask, in_=cf_d[:, CF["dmask"]:CF["dmask"] + 16]), writes=[S_("dmask")])
        P.dma("sync", lambda e: e.dma_start(out=ones128, in_=cf_d[:, CF["ones128"]:CF["ones128"] + 128]), writes=[S_("ones128")])
        P.dma("sync", lambda e: e.dma_start(out=rs1, in_=cf_d[:, CF["rs1f"]:CF["rs1f"] + 1024]), writes=[S_("rs1")])
        P.dma("sync", lambda e: e.dma_start(out=rsm, in_=cf_d[:, CF["rsmf"]:CF["rsmf"] + 1024]), writes=[S_("rsm")])
        st0v = st0_d.rearrange("p (d q par) n -> p d q par n", d=2, q=4)
        for par in range(2):
            P.dma("sync", lambda e, par=par: e.dma_start(out=Cf32[64 * par:64 * par + 64, :, :, :], in_=st0v[:, :, :, par, :]),
                  writes=[S_("Cf32", par)])
        P.op("scalar", lambda e: e.activation(out=Cst, in_=Cf32, func=AF.Copy), reads=[S_("Cf32", 0), S_("Cf32", 1)], writes=[S_("Cst", d_, h_) for d_ in range(2) for h_ in range(8)])
        P.op("vector", lambda e: e.memset(Vx[:, :, :, 128:130], 1.0), writes=[S_("Vones")])

        rt, rres = want_unit(uidx["ml_gate"])
        wg = rt[:, 0:1024].rearrange("p (k n) -> p k n", k=8)
        for hh in range(2):
            sl = slice(hh * 512, (hh + 1) * 512)
            for which in range(2):
                b = next_bank()
                for kc in range(8):
                    P.op("tensor", lambda e, b=b, kc=kc, sl=sl, which=which: e.matmul(
                        banks[b][0:64, :], lhsT=wg[:, kc, which * 64:(which + 1) * 64], rhs=hT[:, kc, sl], start=(kc == 0), stop=(kc == 7)),
                        reads=[rres, ("H", kc, hh)], writes=[("ps", b)], chain=True)
                if which == 0:
                    P.op("scalar", lambda e, b=b, sl=sl: e.activation(out=gi[R, sl], in_=banks[b][0:64, :], func=AF.Identity, bias=vecs[R, VC["gbi"]:VC["gbi"] + 1]),
                         reads=[("ps", b), "vecs"], writes=[S_("gi", hh)])
                else:
                    P.op("scalar", lambda e, b=b, sl=sl: e.activation(out=lf[R, sl], in_=banks[b][0:64, :], func=AF.Exp, scale=-1.0, bias=vecs[R, VC["ngbf"]:VC["ngbf"] + 1]),
                         reads=[("ps", b), "vecs"], writes=[S_("lf", hh)])
        P.op("scalar", lambda e: e.activation(out=lf[R, :], in_=lf[R, :], func=AF.Ln, bias=1.0), reads=[S_("lf", 0), S_("lf", 1)], writes=[S_("lf")])
        P.op("scalar", lambda e: e.mul(out=lf[R, :], in_=lf[R, :], mul=-1.0), reads=[S_("lf")], writes=[S_("lf")])
        P.op("vector", lambda e: e.tensor_tensor_scan(out=bb[R, :], data0=rs1[R, :], data1=lf[R, :], initial=0.0, op0=ALU.mult, op1=ALU.add),
             reads=[S_("lf"), S_("rs1")], writes=[S_("bb", 0), S_("bb", 1)])
        RB = slice(32, 64)
        P.op("vector", lambda e: e.tensor_tensor(out=uu[RB, :], in0=lf[RB, :], in1=bb[RB, :], op=ALU.subtract),
             reads=[S_("lf"), S_("bb", 1)], writes=[S_("uu")])
        for s in range(4):
            seg = slice(s * 256, (s + 1) * 256)
            P.op("vector", lambda e, s=s, seg=seg: e.tensor_scalar(out=uu[RB, seg], in0=uu[RB, seg], scalar1=bb[RB, s * 256 + 255:s * 256 + 256], scalar2=None, op0=ALU.add),
                 reads=[S_("uu"), S_("bb", 1)], writes=[S_("uu")])
        P.op("vector", lambda e: e.tensor_copy(out=bb[RB, :], in_=uu[RB, :]), reads=[S_("uu")], writes=[S_("bb", 1)])
        P.op("vector", lambda e: e.tensor_tensor(out=uu[R, :], in0=gi[R, :], in1=bb[R, :], op=ALU.subtract),
             reads=[S_("gi", 0), S_("gi", 1), S_("bb", 0), S_("bb", 1)], writes=[S_("uu")])
        P.op("vector", lambda e: e.tensor_tensor_scan(out=cm[0:32, :], data0=rsm[0:32, :], data1=uu[0:32, :], initial=-1e30, op0=ALU.add, op1=ALU.max),
             reads=[S_("uu"), S_("rsm")], writes=[S_("cm", 0)])
        v3 = lambda t_: t_.rearrange("p (s t) -> p s t", s=4)
        src = uu
        for k in range(8):
            shf = 1 << k
            dst = ep if k % 2 == 0 else cm
            sv, dv_ = v3(src), v3(dst)
            P.op("vector", lambda e, sv=sv, dv_=dv_, shf=shf: e.tensor_tensor(out=dv_[RB, :, 0:256 - shf], in0=sv[RB, :, 0:256 - shf], in1=sv[RB, :, shf:256], op=ALU.max),
                 reads=[S_("uu"), S_("hs_pp", k)], writes=[S_("hs_pp", k + 1)])
            P.op("vector", lambda e, sv=sv, dv_=dv_, shf=shf: e.tensor_copy(out=dv_[RB, :, 256 - shf:256], in_=sv[RB, :, 256 - shf:256]),
                 reads=[S_("uu"), S_("hs_pp", k), S_("hs_pp", k + 1)], writes=[S_("hs_pp", k + 1)])
            src = dst
        P.op("vector", lambda e: e.tensor_copy(out=junk[:, 0:1], in_=junk[:, 0:1]), reads=[S_("hs_pp", 8)], writes=[S_("cm", 1), "junk"])
        for d in range(2):
            Rd = slice(32 * d, 32 * d + 32)
            order = [0, 1, 2, 3] if d == 0 else [3, 2, 1, 0]
            for n_, s in enumerate(order):
                seg = slice(s * 256, (s + 1) * 256)
                endc = s * 256 + 255 if d == 0 else s * 256
                if n_ == 0:
                    P.op("vector", lambda e, Rd=Rd, s=s: e.tensor_copy(out=mst[Rd, s:s + 1], in_=vecs[Rd, VC["m0"]:VC["m0"] + 1]),
                         reads=["vecs"], writes=[S_("mst", d)])
                else:
                    ps_ = order[n_ - 1]
                    P.op("vector", lambda e, Rd=Rd, s=s, ps_=ps_: e.tensor_tensor(out=mst[Rd, s:s + 1], in0=mend[Rd, ps_:ps_ + 1], in1=vecs[Rd, VC["flag"]:VC["flag"] + 1], op=ALU.mult),
                         reads=["vecs", S_("mend", d)], writes=[S_("mst", d)])
                P.op("vector", lambda e, Rd=Rd, s=s, seg=seg: e.tensor_scalar(out=cm[Rd, seg], in0=cm[Rd, seg], scalar1=mst[Rd, s:s + 1], scalar2=None, op0=ALU.max),
                     reads=[S_("cm", d), S_("mst", d)], writes=[S_("cm", d)])
                P.op("vector", lambda e, Rd=Rd, s=s, endc=endc: e.tensor_tensor(out=mend[Rd, s:s + 1], in0=bb[Rd, endc:endc + 1], in1=cm[Rd, endc:endc + 1], op=ALU.add),
                     reads=[S_("cm", d), S_("bb", d)], writes=[S_("mend", d)])
        P.op("vector", lambda e: e.tensor_scalar(out=nmst[R, :], in0=mst[R, :], scalar1=-1.0, scalar2=-2.0794415416798357, op0=ALU.mult, op1=ALU.add),
             reads=[S_("mst", 0), S_("mst", 1)], writes=[S_("nmst")])
        P.dma("sync", lambda e: e.dma_start(out=stm_d[:, :], in_=mend[R, :]), reads=[S_("mend", 0), S_("mend", 1)], final=True)
        gread = [S_("cm", 0), S_("cm", 1), S_("mst", 0), S_("mst", 1)]
        for s in range(4):
            seg = slice(s * 256, (s + 1) * 256)
            P.op("scalar", lambda e, s=s, seg=seg: e.activation(out=al[R, seg], in_=cm[R, seg], func=AF.Exp, scale=-1.0, bias=mst[R, s:s + 1]),
                 reads=gread, writes=[S_("al", s)])
            P.op("scalar", lambda e, s=s, seg=seg: e.activation(out=ka[R, seg], in_=uu[R, seg], func=AF.Exp, scale=1.0, bias=nmst[R, s:s + 1]),
                 reads=[S_("uu"), S_("nmst")], writes=[S_("ka", s)])
        P.op("vector", lambda e: e.tensor_tensor(out=ep[R, :], in0=bb[R, :], in1=cm[R, :], op=ALU.add),
             reads=gread + [S_("bb", 0), S_("bb", 1)], writes=[S_("ep")])
        P.op("scalar", lambda e: e.activation(out=ep[R, :], in_=ep[R, :], func=AF.Exp, scale=-1.0), reads=[S_("ep")], writes=[S_("ep")])
        alv = al.rearrange("p (s t) -> p s t", s=4)
        alr = [S_("al", s) for s in range(4)]
        P.op("vector", lambda e: e.tensor_copy(out=aend[0:32, :].unsqueeze(2), in_=alv[0:32, :, 255:256]), reads=alr, writes=[S_("aend", 0)])
        P.op("vector", lambda e: e.tensor_copy(out=aend[32:64, :].unsqueeze(2), in_=alv[32:64, :, 0:1]), reads=alr, writes=[S_("aend", 1)])
        Xv = Xd.rearrange("p (s n) -> p s n", s=4)
        P.op("vector", lambda e: e.tensor_tensor(out=Xv[R, :, :], in0=dmask[R, :].unsqueeze(1).to_broadcast([64, 4, 16]),
                                                 in1=aend[R, :].unsqueeze(2).to_broadcast([64, 4, 16]), op=ALU.mult),
             reads=[S_("aend", 0), S_("aend", 1), S_("dmask")], writes=[S_("Xd")])
        P.op("tensor", lambda e: e.matmul(banks[6][:, 0:64], lhsT=ones128[R, :], rhs=Xd[R, :], start=True, stop=True),
             reads=[S_("Xd"), S_("ones128")], writes=[("ps", 6)])
        P.op("vector", lambda e: e.tensor_copy(out=aendb, in_=banks[6][:, 0:64]), reads=[("ps", 6)], writes=[S_("aendb")])
        for tt in range(8):
            b = next_bank((6, 7))
            tsl = slice(tt * 128, (tt + 1) * 128)
            for q, tile_, nm in ((0, al, "al"), (1, ka, "ka"), (2, ep, "ep")):
                rd = [S_(nm, tt // 2)] if nm != "ep" else [S_("ep")]
                P.op("tensor", lambda e, b=b, q=q, tile_=tile_, tsl=tsl: e.transpose(out=banks[b][:, q * 64:(q + 1) * 64], in_=tile_[R, tsl], identity=identf[R, 0:64]),
                     reads=rd + [S_("identf")], writes=[("ps", b)])
            P.op("scalar", lambda e, b=b, tt=tt: e.activation(out=akT[:, tt, :, :], in_=banks[b][:, 0:192].rearrange("p (q r) -> p q r", q=3), func=AF.Copy),
                 reads=[("ps", b)], writes=[S_("akT", tt)])
        rt, rres = want_unit(uidx["ml_v"])
        wv = rt[:, :].rearrange("p (k n) -> p k n", k=8)
        for tt in range(8):
            tsl = slice(tt * 128, (tt + 1) * 128)
            for vh in range(2):
                b = next_bank()
                for kc in range(8):
                    P.op("tensor", lambda e, b=b, kc=kc, tsl=tsl, vh=vh: e.matmul(
                        banks[b][:], lhsT=hT[:, kc, tsl], rhs=wv[:, kc, vh * 512:(vh + 1) * 512], start=(kc == 0), stop=(kc == 7)),
                        reads=[rres, ("H", kc, tt // 4)], writes=[("ps", b)], chain=True)
                P.op("scalar", lambda e, b=b, tt=tt, vh=vh: e.activation(out=Vx[:, tt, 4 * vh:4 * vh + 4, 0:128], in_=banks[b][:].rearrange("p (h n) -> p h n", h=4), func=AF.Copy),
                     reads=[("ps", b)], writes=[S_("Vx", tt, vh)])
                pump(2)
        finish_pending()
        P.fence("S")

        seq = [(0, sg_) for sg_ in (0, 1, 2, 3)] + [(1, sg_) for sg_ in (3, 2, 1, 0)]
        ust = {}

        def emit_proj(idx):
            d, seg = seq[idx]
            sb_ = idx % 2
            if idx == 0 or idx == 4:
                rt, rres = want_unit(uidx["ml_qk"] if d == 0 else uidx["ml_qk2"])
                ust["wqk"], ust["rres"] = rt[:, :].rearrange("p (k n) -> p k n", k=8), rres
                if d == 1:
                    rto, rreso = want_unit(uidx["ml_o"], la=0)
                    ust["wo"], ust["rreso"] = rto[:, :].rearrange("p (k n) -> p k n", k=8), rreso
            wqk, rres = ust["wqk"], ust["rres"]
            qT, kT = qTs[sb_], kTs[sb_]
            for j in range(2):
                tt = 2 * seg + j
                tsl = slice(tt * 128, (tt + 1) * 128)
                qk = qkt[2 * sb_ + j]
                for part in range(2):
                    b = next_bank()
                    for kc in range(8):
                        P.op("tensor", lambda e, b=b, kc=kc, tsl=tsl, part=part, wqk=wqk: e.matmul(
                            banks[b][:], lhsT=hT[:, kc, tsl], rhs=wqk[:, kc, part * 512:(part + 1) * 512], start=(kc == 0), stop=(kc == 7)),
                            reads=[rres, ("H", kc, tt // 4)], writes=[("ps", b)], chain=True)
                    P.op("vector", lambda e, b=b, qk=qk, part=part, tt=tt, d=d: e.tensor_tensor(
                        out=qk[:, part * 512:(part + 1) * 512].rearrange("p (h n) -> p h n", h=8),
                        in0=banks[b][:].rearrange("p (h n) -> p h n", h=8),
                        in1=akT[:, tt, part, 32 * d:32 * d + 8].unsqueeze(2).to_broadcast([128, 8, 64]), op=ALU.mult),
                        reads=[("ps", b), S_("akT", tt)], writes=[S_("qkt", 2 * sb_ + j, part)])
            for part, dst, nm in ((0, qT, "qT"), (1, kT, "kT")):
                b = next_bank((6, 7))
                pv = banks[b][:].bitcast(BF16).rearrange("p (q t) -> p q t", q=4)
                for pair in range(4):
                    for j in range(2):
                        qk = qkt[2 * sb_ + j]
                        P.op("tensor", lambda e, pv=pv, pair=pair, j=j, qk=qk, part=part: e.transpose(
                            out=pv[:, pair, j * 128:(j + 1) * 128], in_=qk[:, part * 512 + pair * 128:part * 512 + (pair + 1) * 128], identity=ident_bf),
                            reads=[S_("qkt", 2 * sb_ + j, part), "cbf"], writes=[("ps", b)])
                P.op("scalar", lambda e, pv=pv, dst=dst: e.activation(out=dst, in_=pv, func=AF.Copy),
                     reads=[("ps", b)], writes=[S_(nm, sb_)])

        emit_proj(0)
        for idx in range(8):
            if True:
                d, seg = seq[idx]
                sb_ = idx % 2
                qT, kT = qTs[sb_], kTs[sb_]
                mask = maskF if d == 0 else maskB
                if d == 1:
                    wo, rreso = ust["wo"], ust["rreso"]
                if d == 0:
                    blocks = [(0, 0), (0, 1), (1, 1)]
                else:
                    blocks = [(0, 0), (1, 0), (1, 1)]

                def emit_S(h, kT=kT, qT=qT, blocks=blocks, mask=mask, sb_=sb_):
                    pair, pb = h // 2, 64 * (h % 2)
                    PB = slice(pb, pb + 64)
                    kTh, qTh = kT[PB, pair, :], qT[PB, pair, :]
                    sbk = next_bank((0, 1))
                    Sb = banks[sbk]
                    for k_, (ci, cj) in enumerate(blocks):
                        P.op("tensor", lambda e, Sb=Sb, k_=k_, ci=ci, cj=cj, kTh=kTh, qTh=qTh: e.matmul(
                            Sb[:, k_ * 128:(k_ + 1) * 128], lhsT=kTh[:, ci * 128:(ci + 1) * 128], rhs=qTh[:, cj * 128:(cj + 1) * 128], start=True, stop=True),
                            reads=[S_("kT", sb_), S_("qT", sb_)], writes=[("ps", sbk)])
                    st = St[h % 2]
                    Sv = Sb[:, 0:384].rearrange("p (b t) -> p b t", b=3)
                    P.op("vector", lambda e, st=st, Sv=Sv, mask=mask: e.tensor_tensor(out=st[:, 0:3:2, :], in0=Sv[:, 0:3:2, :],
                                                                                    in1=mask.unsqueeze(1).to_broadcast([128, 2, 128]), op=ALU.mult),
                         reads=[("ps", sbk), "cbf"], writes=[S_("St", h % 2, 0)])
                    P.op("scalar", lambda e, st=st, Sv=Sv: e.activation(out=st[:, 1, :], in_=Sv[:, 1, :], func=AF.Copy),
                         reads=[("ps", sbk)], writes=[S_("St", h % 2, 1), ("ps", sbk)])

                emit_S(0)
                if idx + 1 < 8:
                    emit_proj(idx + 1)
                for h in range(8):
                    pair, pb = h // 2, 64 * (h % 2)
                    PB = slice(pb, pb + 64)
                    kTh, qTh = kT[PB, pair, :], qT[PB, pair, :]
                    st = St[h % 2]
                    if h < 7:
                        emit_S(h + 1)
                    nbk = next_bank((2, 3))
                    Nb = banks[nbk]
                    strd = [S_("St", h % 2, 0), S_("St", h % 2, 1)]
                    for j in range(2):
                        terms = [(k_, ci) for k_, (ci, cj) in enumerate(blocks) if cj == j]
                        for n_, (k_, ci) in enumerate(terms):
                            P.op("tensor", lambda e, Nb=Nb, j=j, k_=k_, ci=ci, st=st, h=h, n_=n_, seg=seg: e.matmul(
                                Nb[:, j * 130:(j + 1) * 130], lhsT=st[:, k_, :], rhs=Vx[:, 2 * seg + ci, h, :], start=(n_ == 0), stop=False),
                                reads=strd + [S_("Vx", 2 * seg + ci, h // 4), S_("Vones")], writes=[("ps", nbk)], chain=(n_ > 0))
                        P.op("tensor", lambda e, Nb=Nb, j=j, qTh=qTh, PB=PB, pair=pair, d=d: e.matmul(
                            Nb[:, j * 130:(j + 1) * 130], lhsT=qTh[:, j * 128:(j + 1) * 128], rhs=Cst[PB, d, pair, :], start=False, stop=True),
                            reads=[S_("qT", sb_), S_("Cst", d, h)], writes=[("ps", nbk)], chain=True)
                    ebk = next_bank((4, 5))
                    Eb = banks[ebk]
                    for j in range(2):
                        qk = qkt[2 * sb_ + j]
                        P.op("tensor", lambda e, Eb=Eb, PB=PB, qk=qk, h=h, j=j, seg=seg: e.matmul(
                            Eb[PB, 0:130], lhsT=qk[:, 512 + h * 64:512 + (h + 1) * 64], rhs=Vx[:, 2 * seg + j, h, :], start=(j == 0), stop=False),
                            reads=[S_("qkt", 2 * sb_ + j, 1), S_("Vx", 2 * seg + j, h // 4), S_("Vones")], writes=[("ps", ebk)], chain=(j > 0))
                    P.op("tensor", lambda e, Eb=Eb, PB=PB, pair=pair, d=d: e.matmul(
                        Eb[PB, 0:130], lhsT=ident_bf[PB, pb:pb + 64], rhs=Cst[PB, d, pair, :], start=False, stop=True),
                        reads=[S_("Cst", d, h), "cbf"], writes=[("ps", ebk)], chain=True)
                    Nv = Nb[:, 0:260].rearrange("p (j n) -> p j n", j=2)
                    r_ = 32 * d + h
                    dnk = dn[h % 2]
                    P.op("scalar", lambda e, dnk=dnk, Nv=Nv: e.activation(out=dnk.unsqueeze(2), in_=Nv[:, :, 128:129], func=AF.Abs),
                         reads=[("ps", nbk)], writes=[S_("dn", h % 2)])
                    P.op("vector", lambda e, dnk=dnk, seg=seg, r_=r_: e.tensor_tensor(out=dnk.unsqueeze(2), in0=dnk.unsqueeze(2), in1=akT[:, 2 * seg:2 * seg + 2, 2, r_:r_ + 1], op=ALU.max),
                         reads=[S_("dn", h % 2), S_("akT", 2 * seg), S_("akT", 2 * seg + 1)], writes=[S_("dn", h % 2)])
                    P.op("vector", lambda e, dnk=dnk: e.reciprocal(out=dnk, in_=dnk), reads=[S_("dn", h % 2)], writes=[S_("dn", h % 2)])
                    for j in range(2):
                        tt = 2 * seg + j
                        hsl = slice(h * 128, (h + 1) * 128)
                        if d == 0:
                            P.op("scalar", lambda e, Nv=Nv, j=j, tt=tt, hsl=hsl, dnk=dnk: e.activation(out=hf[:, tt, hsl], in_=Nv[:, j, 0:128], func=AF.Copy, scale=dnk[:, j:j + 1]),
                                 reads=[("ps", nbk), S_("dn", h % 2)], writes=[S_("hf", tt, h)])
                        else:
                            P.op("vector", lambda e, Nv=Nv, j=j, tt=tt, hsl=hsl, dnk=dnk: e.scalar_tensor_tensor(out=hs[:, j, hsl], in0=Nv[:, j, 0:128], scalar=dnk[:, j:j + 1], in1=hf[:, tt, hsl],
                                                                                                            op0=ALU.mult, op1=ALU.add),
                                 reads=[("ps", nbk), S_("dn", h % 2), S_("hf", tt, h)], writes=[S_("hs", j, h)])
                    ef = Ef[h % 2]
                    col = seg * 16 + d * 8 + h
                    P.op("scalar", lambda e, ef=ef, Eb=Eb, PB=PB, col=col: e.activation(out=ef[PB, :], in_=Eb[PB, 0:130], func=AF.Copy, scale=aendb[PB, col:col + 1]),
                         reads=[("ps", ebk), S_("aendb")], writes=[S_("Ef", h % 2)])
                    P.dma("sync", lambda e, ef=ef, PB=PB, col=col: e.dma_start(out=stE_d[:, col, :], in_=ef[PB, :]), reads=[S_("Ef", h % 2)], final=True)
                    P.op("scalar", lambda e, ef=ef, PB=PB, pair=pair, d=d: e.activation(out=Cst[PB, d, pair, :], in_=ef[PB, :], func=AF.Copy, scale=vecs[PB, VC["flag"]:VC["flag"] + 1]),
                         reads=[S_("Ef", h % 2), "vecs"], writes=[S_("Cst", d, h)])
                if d == 1:
                    for j in range(2):
                        tt = 2 * seg + j
                        tsl = slice(tt * 128, (tt + 1) * 128)
                        hsj = hs[:, j, :]
                        hsv = hsj.rearrange("p (h n) -> p h n", h=8)
                        hrd = [S_("hs", j, h) for h in range(8)]
                        P.op("vector", lambda e, hsv=hsv: e.reduce_sum(out=s1, in_=hsv, axis=AX.X), reads=hrd, writes=[S_("s1")])
                        for h in range(8):
                            P.op("scalar", lambda e, hsj=hsj, h=h: e.activation(out=junk2[:, :], in_=hsj[:, h * 128:(h + 1) * 128], func=AF.Square, accum_out=s2[:, h:h + 1]),
                                 reads=hrd, writes=[S_("s2", h), "junk2"])
                        P.op("vector", lambda e: e.tensor_scalar(out=mean_, in0=s1, scalar1=1.0 / 128, scalar2=None, op0=ALU.mult), reads=[S_("s1")], writes=[S_("mean")])
                        P.op("vector", lambda e: e.tensor_tensor(out=s1, in0=mean_, in1=mean_, op=ALU.mult), reads=[S_("mean")], writes=[S_("s1")])
                        P.op("vector", lambda e: e.scalar_tensor_tensor(out=rstd_, in0=s2, scalar=1.0 / 128, in1=s1, op0=ALU.mult, op1=ALU.subtract),
                             reads=[S_("s2", h) for h in range(8)] + [S_("s1")], writes=[S_("rstd")])
                        P.op("vector", lambda e: e.tensor_scalar(out=rstd_, in0=rstd_, scalar1=LN_EPS, scalar2=None, op0=ALU.add), reads=[S_("rstd")], writes=[S_("rstd")])
                        P.op("scalar", lambda e: e.activation(out=rstd_, in_=rstd_, func=AF.Sqrt), reads=[S_("rstd")], writes=[S_("rstd")])
                        P.op("vector", lambda e: e.reciprocal(out=rstd_, in_=rstd_), reads=[S_("rstd")], writes=[S_("rstd")])
                        P.op("vector", lambda e, hsv=hsv: e.tensor_tensor(out=hsv, in0=hsv, in1=mean_.unsqueeze(2).to_broadcast([128, 8, 128]), op=ALU.subtract),
                             reads=hrd + [S_("mean")], writes=hrd)
                        P.op("vector", lambda e, hsv=hsv: e.tensor_tensor(out=hsv, in0=hsv, in1=rstd_.unsqueeze(2).to_broadcast([128, 8, 128]), op=ALU.mult),
                             reads=hrd + [S_("rstd")], writes=hrd)
                        for oh in range(2):
                            b = next_bank((0, 1))
                            for kc in range(8):
                                P.op("tensor", lambda e, b=b, kc=kc, tsl=tsl, oh=oh, wo=wo: e.matmul(
                                    banks[b][:], lhsT=hT[:, kc, tsl], rhs=wo[:, kc, oh * 512:(oh + 1) * 512], start=(kc == 0), stop=(kc == 7)),
                                    reads=[rreso, ("H", kc, tt // 4)], writes=[("ps", b)], chain=True)
                            P.op("scalar", lambda e, b=b: e.activation(out=sg, in_=banks[b][:], func=AF.Sigmoid), reads=[("ps", b)], writes=[S_("sg")])
                            P.op("vector", lambda e, hsj=hsj, oh=oh: e.tensor_tensor(out=hgt[:, oh * 512:(oh + 1) * 512], in0=hsj[:, oh * 512:(oh + 1) * 512], in1=sg, op=ALU.mult),
                                 reads=hrd + [S_("sg")], writes=[S_("hgt", oh)])
                        b = next_bank((6, 7))
                        pv = banks[b][:].bitcast(BF16)
                        for c in range(8):
                            P.op("tensor", lambda e, pv=pv, c=c: e.transpose(out=pv[:, c * 128:(c + 1) * 128], in_=hgt[:, c * 128:(c + 1) * 128], identity=ident_bf),
                                 reads=[S_("hgt", c // 4), "cbf"], writes=[("ps", b)])
                        for c in range(8):
                            P.op("scalar", lambda e, pv=pv, c=c, tsl=tsl: e.activation(out=hnT[:, c, tsl], in_=pv[:, c * 128:(c + 1) * 128], func=AF.Copy,
                                                                                      scale=vecs[:, VC["mlng"] + c:VC["mlng"] + c + 1]),
                                 reads=[("ps", b), "vecs"], writes=[S_("hnT", c, tt)])
        out_proj(i, "ml_out", hnT, lambda kc, hh: [S_("hnT", kc, t_) for t_ in range(4 * hh, 4 * hh + 4)], lambda dc: g1(i, dc))

    P.dma("sync", lambda e: e.dma_start(out=xa[:, 0:4, :], in_=xT_d[:, 0:4, :]), writes=[("xa", c, h) for c in range(4) for h in range(2)])
    P.dma("sync", lambda e: e.dma_start(out=xa[:, 4:8, :], in_=xT_d[:, 4:8, :]), writes=[("xa", c, h) for c in range(4, 8) for h in range(2)])
    posv = sview(0, 32768, F32, "p (c t) -> p c t", c=8)
    P.dma("sync", lambda e: e.dma_start(out=posv, in_=posT_d[:, :, :]), writes=[("S", "pos")])
    layers = cfg["layers"]
    it0 = mod_steps(layers[0])
    for _ in range(16):
        mod_step(*next(it0))
    mod_finish_a(layers[0])
    pend["it"], pend["layer"], pend["done"] = it0, layers[0], False
    for c in range(8):
        for hf in range(2):
            sl = slice(hf * 512, (hf + 1) * 512)
            P.op("vector", lambda e, c=c, sl=sl: e.tensor_tensor(out=xa[:, c, sl], in0=xa[:, c, sl], in1=posv[:, c, sl], op=ALU.add),
                 reads=[("xa", c, hf), ("S", "pos")], writes=[("xa", c, hf)])
            P.op("scalar", lambda e, c=c, sl=sl: e.mul(out=xa[:, c, sl], in_=xa[:, c, sl], mul=ALPHA),
                 reads=[("xa", c, hf)], writes=[("xa", c, hf)])
    P.fence("S")
    modulate(layers[0], 0)
    for li, i in enumerate(layers):
        kind = i % 4
        nxt = layers[li + 1] if li + 1 < len(layers) else None
        steps = mod_steps(nxt) if nxt is not None else iter(())
        if kind != 0 or kind not in cfg["mixers"]:
            finish_pending()
        if kind in cfg["mixers"]:
            if kind == 0:
                mixer_mlstm(i)
            elif kind == 1:
                mixer_conv(i)
            elif kind == 2:
                mixer_pool(i)
            else:
                mixer_fourier(i)
        P.fence("S")
        for _ in ln_phases(i, 0, 0, True, i, 1):
            pass
        lngen = ln_phases(i, 0, 1, True, i, 1)
        if not cfg["mlp"]:
            for _ in lngen:
                pass
        if cfg["mlp"]:
            mlp(i, steps, nxt, lngen)
        else:
            for st in steps:
                mod_step(*st)
            if nxt is not None:
                mod_finish_a(nxt)
                mod_finish_b(nxt)
            layer_norm(i, 1, do_h=(nxt is not None), nexti=nxt, nextj=0)
        P.fence("S")
    for c in range(8):
        P.dma("sync", lambda e, c=c: e.dma_start(out=yT_d[:, c, :], in_=xa[:, c, :]), reads=[("xa", c, 0), ("xa", c, 1)], final=True)
    P.finish()
    P.build()
    return nc


def from_mlstm(i):
    raise NotImplementedError


def make_in_maps(inp):
    inp = {k: np.asarray(v) for k, v in inp.items()}
    wall = build_wall(inp)
    wmodT = np.ascontiguousarray(np.asarray(inp["w_mod"], np.float32).transpose(0, 2, 1).reshape(4 * 6144, 1024))
    cb = const_bf()
    cf = const_f32()
    mixP = mix_mats(True).reshape(6 * 128, UNITC)
    mixS = mix_mats(False).reshape(6 * 128, UNITC)
    pos = grid_pos_embed_np(16)
    G, bi, bf = gate_layout(inp["ml_w_gate"][0], inp["ml_b_gate"][0])
    maps = []
    for core in range(8):
        prompt = core < 4
        vec = np.zeros((128, NV), np.float32)
        for i in range(4):
            bm = np.asarray(inp["b_mod"][i], np.float32).reshape(48, 128).T
            vec[:, VC["bmod"] + i * 48:VC["bmod"] + (i + 1) * 48] = bm
            for j in range(2):
                vec[:, VC["lng"] + (i * 2 + j) * 8:VC["lng"] + (i * 2 + j + 1) * 8] = fm(inp["ln_g"][i, j])
                vec[:, VC["lnb"] + (i * 2 + j) * 8:VC["lnb"] + (i * 2 + j + 1) * 8] = fm(inp["ln_b"][i, j])
        for k in range(3):
            vec[:, VC["convw"] + k * 8:VC["convw"] + (k + 1) * 8] = fm(inp["sc_conv_w"][0, k])
        vec[:, VC["plb"]:VC["plb"] + 8] = fm(np.asarray(inp["pl_b"][0]).reshape(-1))
        vec[:, VC["pls"]:VC["pls"] + 8] = fm(inp["pl_scale"][0])
        vec[:, VC["ftb"]:VC["ftb"] + 8] = fm(inp["ft_b_out"][0])
        vec[:, VC["flag"]] = 0.0 if prompt else 1.0
        vec[:, VC["bflag"]] = 1.0 if prompt else 0.0
        vec[:, VC["gbi"]] = bi
        vec[:, VC["ngbf"]] = -bf
        vec[:, VC["mlng"]:VC["mlng"] + 8] = fm(inp["ml_norm_g"][0])
        st0 = np.zeros((64, 16, 130), np.float32)
        if prompt:
            x = np.asarray(inp["x_prompt"][4 * core:4 * core + 4], np.float32).reshape(1024, 1024)
            posT = np.zeros((128, 8, 1024), np.float32)
            cond = np.asarray(inp["c_ctx"], np.float32)
        else:
            b = core - 4
            x = np.asarray(inp["x_sample"][b], np.float32)
            posT = to_fm3(pos)
            cond = np.asarray(inp["c"][b], np.float32)
            C0 = np.asarray(inp["state_C"][b, 0], np.float32)
            n0 = np.asarray(inp["state_n"][b, 0], np.float32)
            m0 = np.asarray(inp["state_m"][b, 0], np.float32)
            st0[:, :, 0:128] = C0.reshape(16, 64, 128).transpose(1, 0, 2)
            st0[:, :, 128] = n0.reshape(16, 64).T
            for d in range(2):
                vec[32 * d:32 * d + 8, VC["m0"]] = m0[d]
        maps.append({
            "xT": to_fm3(x), "posT": posT, "condb": np.ascontiguousarray(np.broadcast_to(cond[None, :], (128, 1024))),
            "vecs": vec, "wmodT": wmodT, "wall": wall, "mixm": mixP if prompt else mixS, "cbf": cb, "cf": cf,
            "st0": st0, "ngb": np.ascontiguousarray(np.broadcast_to(np.asarray(inp["ml_norm_g"][0], np.float32)[None, :], (128, 1024))),
        })
    return maps


_NC_CACHE = {}


def kernel(**inputs):
    key = "main"
    if key not in _NC_CACHE:
        _NC_CACHE[key] = build_program(CFG)
    nc = _NC_CACHE[key]
    maps = make_in_maps(inputs)
    res = run_bass_kernel_spmd(nc, maps, core_ids=list(range(8)))
    outs = res.results
    y_prompt = np.zeros((16, 256, 1024), np.float32)
    y_sample = np.zeros((4, 1024, 1024), np.float32)
    new_C = np.zeros((16, 1, 2, 8, 64, 128), np.float32)
    new_n = np.zeros((16, 1, 2, 8, 64), np.float32)
    new_m = np.zeros((16, 1, 2, 8), np.float32)
    for core in range(8):
        y = from_fm3(np.asarray(outs[core]["yT"], np.float32))
        if core < 4:
            y_prompt[4 * core:4 * core + 4] = y.reshape(4, 256, 1024)
            stE = np.asarray(outs[core]["stE"], np.float32).reshape(64, 4, 2, 8, 130)
            stm = np.asarray(outs[core]["stm"], np.float32)
            for seg in range(4):
                q = 4 * core + seg
                new_C[q, 0] = stE[:, seg, :, :, 0:128].transpose(1, 2, 0, 3)
                new_n[q, 0] = stE[:, seg, :, :, 128].transpose(1, 2, 0)
                for d in range(2):
                    new_m[q, 0, d] = stm[32 * d:32 * d + 8, seg]
        else:
            y_sample[core - 4] = y
    return (y_prompt, y_sample, new_C, new_n, new_m)
```

```python
import contextlib
import numpy as np
import ml_dtypes
import concourse.bass as bass
import concourse.mybir as mybir
from concourse.bass_utils import run_bass_kernel_spmd

F32 = mybir.dt.float32
BF16 = mybir.dt.bfloat16
AF = mybir.ActivationFunctionType
ALU = mybir.AluOpType
AX = mybir.AxisListType

D = 1024
T = 1024
DEPTH = 4
ALPHA = 8.0 ** 0.25
LN_EPS = 1e-5
RING = 3
UNITC = 8192
ENGS = ["sync", "tensor", "vector", "scalar", "gpsimd"]

CFG = {"layers": [0, 1, 2, 3], "mixers": {0, 1, 2, 3}, "mlp": True, "same_engine_sync": True}


class Prog:
    def __init__(self, nc, n_dma_sems=28, same_engine_sync=True):
        self.nc = nc
        self.stack = contextlib.ExitStack()
        self.streams = {e: [] for e in ENGS}
        self.count = {e: 0 for e in ENGS}
        self.sem = {e: self.stack.enter_context(nc.semaphore("prog_" + e)) for e in ENGS}
        self.dsem = [self.stack.enter_context(nc.semaphore("dma_%d" % i)) for i in range(n_dma_sems)]
        self.dcount = [0] * n_dma_sems
        self.dnext = 0
        self.dnext_sw = 0
        self.res = {}
        self.fences = {}
        self.waited = {e: {} for e in ENGS}
        self.same_engine_sync = same_engine_sync
        self.final_events = []
        self.last_nonchain = {}

    def sb(self, name, shape, dtype):
        return self.stack.enter_context(self.nc.sbuf_tensor("S_" + name, shape, dtype))

    def ps(self, name, shape, dtype=F32):
        return self.stack.enter_context(self.nc.psum_tensor("P_" + name, shape, dtype))

    def _state(self, key):
        st = self.res.get(key)
        if st is None and isinstance(key, tuple) and key[0] in self.fences:
            st = {"w": None, "r": list(self.fences[key[0]]), "fw": list(self.fences[key[0]])}
            self.res[key] = st
        return st

    def fence(self, region):
        evs = list(self.fences.get(region, []))
        for key in [k for k in self.res if isinstance(k, tuple) and k[0] == region]:
            st = self.res.pop(key)
            if st["w"] is not None:
                evs.append(st["w"])
            evs.extend(st["r"])
        best = {}
        for key, val, src in evs:
            if key not in best or best[key][1] < val:
                best[key] = (key, val, src)
        self.fences[region] = list(best.values())

    def _deps(self, eng, reads, writes, chain=False):
        evs = []
        for r in reads:
            st = self._state(r)
            if st and st["w"] is not None:
                evs.append(st["w"])
            if st and st.get("fw"):
                evs.extend(st["fw"])
        for w in writes:
            st = self._state(w)
            if st:
                if st["w"] is not None:
                    evs.append(st["w"])
                evs.extend(st["r"])
        best = {}
        for key, val, src in evs:
            if src == eng and (chain or not self.same_engine_sync):
                continue
            if self.waited[eng].get(key, 0) >= val:
                continue
            best[key] = max(best.get(key, 0), val)
        for k, v in best.items():
            self.waited[eng][k] = v
        return list(best.items())

    def _semh(self, key):
        return self.sem[key] if isinstance(key, str) else self.dsem[key]

    def _commit(self, ev, reads, writes):
        for r in reads:
            st = self._state(r)
            if st is None:
                st = {"w": None, "r": []}
                self.res[r] = st
            st["r"].append(ev)
            if len(st["r"]) > 24:
                best = {}
                for key, val, src in st["r"]:
                    if key not in best or best[key][1] < val:
                        best[key] = (key, val, src)
                st["r"] = list(best.values())
        for w in writes:
            self.res[w] = {"w": ev, "r": []}

    def op(self, eng, fn, reads=(), writes=(), chain=False):
        waits = self._deps(eng, reads, writes, chain)
        if chain:
            ln = self.last_nonchain.get(eng, 0)
            if ln > 0 and self.waited[eng].get(eng, 0) < ln:
                self.waited[eng][eng] = ln
                waits.append((eng, ln))
        else:
            self.last_nonchain[eng] = self.count[eng] + 1
        self.count[eng] += 1
        ev = (eng, self.count[eng], eng)
        self.streams[eng].append((waits, fn, (eng, 1)))
        self._commit(ev, reads, writes)
        return ev

    def dma(self, q, fn, reads=(), writes=(), final=False):
        nsw = 8
        if q == "gpsimd":
            j = self.dnext_sw
            self.dnext_sw = (self.dnext_sw + 1) % nsw
        else:
            j = nsw + self.dnext
            self.dnext = (self.dnext + 1) % (len(self.dsem) - nsw)
        waits = self._deps(q, reads, writes)
        if self.dcount[j] > 0 and self.waited[q].get(j, 0) < self.dcount[j]:
            self.waited[q][j] = self.dcount[j]
            waits.append((j, self.dcount[j]))
        self.dcount[j] += 16
        ev = (j, self.dcount[j], "dma")
        self.streams[q].append((waits, fn, (j, 16)))
        self._commit(ev, reads, writes)
        if final:
            self.final_events.append(ev)
        return ev

    def finish(self):
        for q in ("sync",):
            waits = []
            for key, val, _ in self.final_events:
                if self.waited[q].get(key, 0) >= val:
                    continue
                self.waited[q][key] = val
                waits.append((key, val))
            self.streams[q].append((waits, None, None))

    def build(self):
        nc = self.nc
        with nc.Block() as block:
            def mk(ename):
                def body(e):
                    for waits, fn, inc in self.streams[ename]:
                        for key, val in waits:
                            e.wait_ge(self._semh(key), val)
                        if fn is not None:
                            ins = fn(e)
                            ins.then_inc(self._semh(inc[0]), inc[1])
                return body
            block.sync(mk("sync"))
            block.tensor(mk("tensor"))
            block.vector(mk("vector"))
            block.scalar(mk("scalar"))
            block.gpsimd(mk("gpsimd"))
        self.stack.close()


def fm(v):
    return np.ascontiguousarray(np.asarray(v, np.float32).reshape(8, 128).T)


def kmajor(W, n0, ncols=1024):
    W = np.asarray(W, np.float32)
    blk = W[:, n0:n0 + ncols].reshape(8, 128, ncols).transpose(1, 0, 2).reshape(128, 8 * ncols)
    out = np.zeros((128, UNITC), np.float32)
    out[:, :8 * ncols] = blk
    return out


def w2strip(W2, s):
    W2 = np.asarray(W2, np.float32)
    return np.ascontiguousarray(W2[:, s * 256:(s + 1) * 256].reshape(32, 128, 256).transpose(1, 0, 2).reshape(128, UNITC))


VC = {}
_off = 0
for _name, _n in [("bmod", 4 * 48), ("lng", 64), ("lnb", 64), ("convw", 24), ("plb", 8), ("pls", 8), ("ftb", 8),
                  ("flag", 1), ("bflag", 1), ("gbi", 1), ("ngbf", 1), ("m0", 1), ("mlng", 8)]:
    VC[_name] = _off
    _off += _n
NV = _off

CB = {}
_off = 0
for _name, _n in [("ident", 128), ("ones", 128), ("maskF", 128), ("maskB", 128), ("cwsw", 1024)]:
    CB[_name] = _off
    _off += _n
NCB = _off

CF = {}
_off = 0
for _name, _n in [("identf", 128), ("rs1f", 1024), ("rsmf", 1024), ("dmask", 16), ("ones128", 128)]:
    CF[_name] = _off
    _off += _n
NCF = _off


def unit_plan():
    plan = []
    for i in range(DEPTH):
        kind = i % 4
        if kind == 0:
            plan += [("ml_gate", 1024), ("ml_v", UNITC), ("ml_qk", UNITC), ("ml_qk2", UNITC), ("ml_o", UNITC), ("ml_out", UNITC)]
        elif kind == 1:
            plan += [("sc_u", UNITC), ("sc_cg", UNITC), ("sc_bg", UNITC), ("sc_out", UNITC)]
        elif kind == 2:
            plan += [("pl_w", 2048)]
        else:
            plan += [("ft_out", UNITC)]
        plan += [("w1_%d_%d" % (i, b), UNITC) for b in range(4)]
        plan += [("w2_%d_%d" % (i, s), UNITC) for s in range(4)]
        plan += [("w2b_%d_%d" % (i, s), UNITC) for s in range(4)]
    return plan


def gate_layout(w_gate, b_gate):
    wg = np.asarray(w_gate, np.float32).reshape(1024, 2, 2, 8)
    bg = np.asarray(b_gate, np.float32).reshape(2, 2, 8)
    G = np.zeros((1024, 128), np.float32)
    bi = np.zeros((128,), np.float32)
    bf = np.full((128,), 30.0, np.float32)
    for d in range(2):
        G[:, 32 * d:32 * d + 8] = wg[:, d, 0, :]
        G[:, 64 + 32 * d:64 + 32 * d + 8] = wg[:, d, 1, :]
        bi[32 * d:32 * d + 8] = bg[d, 0]
        bf[32 * d:32 * d + 8] = bg[d, 1]
    return G, bi, bf


def build_wall(inp):
    units = []
    for name, ncols in unit_plan():
        if name == "ml_gate":
            G, _, _ = gate_layout(inp["ml_w_gate"][0], inp["ml_b_gate"][0])
            units.append(kmajor(G, 0, 128))
        elif name in ("ml_qk", "ml_qk2"):
            units.append(kmajor(inp["ml_w_in"][0], 0))
        elif name == "ml_v":
            units.append(kmajor(inp["ml_w_in"][0], 1024))
        elif name == "ml_o":
            units.append(kmajor(inp["ml_w_in"][0], 2048))
        elif name == "ml_out":
            units.append(kmajor(inp["ml_w_out"][0], 0))
        elif name == "sc_u":
            units.append(kmajor(inp["sc_w_in"][0], 2048))
        elif name == "sc_cg":
            units.append(kmajor(inp["sc_w_in"][0], 1024))
        elif name == "sc_bg":
            units.append(kmajor(inp["sc_w_in"][0], 0))
        elif name == "sc_out":
            units.append(kmajor(inp["sc_w_out"][0], 0))
        elif name == "pl_w":
            w = np.asarray(inp["pl_w"][0], np.float32)
            blk = w.reshape(4, 2, 128, 256).transpose(2, 0, 1, 3).reshape(128, 2048)
            u = np.zeros((128, UNITC), np.float32)
            u[:, :2048] = blk
            units.append(u)
        elif name == "ft_out":
            units.append(kmajor(inp["ft_w_out"][0], 0))
        elif name.startswith("w1_") or name.startswith("w1b_"):
            _, i, b = name.split("_")
            units.append(kmajor(inp["mlp_w1"][int(i)], int(b) * 1024))
        elif name.startswith("w2_") or name.startswith("w2b_"):
            _, i, s = name.split("_")
            units.append(w2strip(inp["mlp_w2"][int(i)], int(s)))
        else:
            raise KeyError(name)
    return np.concatenate(units, axis=0)


def grid_pos_embed_np(rows):
    rr, cc = np.meshgrid(np.arange(rows, dtype=np.float32), np.arange(64, dtype=np.float32), indexing="ij")
    quarter = D // 4
    omega = (1.0 / (10000.0 ** (np.arange(quarter, dtype=np.float32) / np.float32(quarter)))).astype(np.float32)
    er = rr.reshape(-1, 1) * omega
    ec = cc.reshape(-1, 1) * omega
    return np.concatenate([np.sin(er), np.cos(er), np.sin(ec), np.cos(ec)], axis=-1).astype(np.float32)


def to_fm3(x):
    return np.ascontiguousarray(np.asarray(x, np.float32).T.reshape(8, 128, 1024).transpose(1, 0, 2))


def from_fm3(y):
    return np.ascontiguousarray(y.transpose(1, 0, 2).reshape(1024, 1024).T)


def mix_mats(prompt):
    S = 256 if prompt else 1024
    nseq = T // S
    mats = []
    t = np.arange(S)
    for win in (2, 4, 8, 16):
        lo = np.clip(t - win // 2, 0, S)
        hi = np.clip(t + win - win // 2, 0, S)
        A = np.zeros((S, S), np.float64)
        for tt in range(S):
            A[tt, lo[tt]:hi[tt]] = 1.0 / (hi[tt] - lo[tt])
        A -= np.eye(S)
        B = A.T
        full = np.zeros((T, T), np.float64)
        for q in range(nseq):
            full[q * S:(q + 1) * S, q * S:(q + 1) * S] = B
        mats.append(full)
    ang = 2.0 * np.pi * np.outer(t, t) / S
    nrm = 1.0 / np.sqrt(S * 256.0)
    for M in (np.cos(ang) * nrm, -np.sin(ang) * nrm):
        full = np.zeros((T, T), np.float64)
        for q in range(nseq):
            full[q * S:(q + 1) * S, q * S:(q + 1) * S] = M
        mats.append(full)
    out = np.stack([m.reshape(8, 128, T).transpose(1, 0, 2).reshape(128, UNITC) for m in mats], axis=0)
    return out.astype(ml_dtypes.bfloat16)


def const_bf():
    c = np.zeros((128, NCB), np.float32)
    c[:, CB["ident"]:CB["ident"] + 128] = np.eye(128)
    c[:, CB["ones"]:CB["ones"] + 128] = 1.0
    s = np.arange(128)[:, None]
    t = np.arange(128)[None, :]
    c[:, CB["maskF"]:CB["maskF"] + 128] = (s <= t)
    c[:, CB["maskB"]:CB["maskB"] + 128] = (s >= t)
    cc = np.arange(256)
    ang = 2.0 * np.pi * np.outer(cc, cc) / 256.0
    cw = np.cos(ang)
    sw = np.sin(ang)
    both = np.concatenate([cw, sw], axis=1)
    c[:, CB["cwsw"]:CB["cwsw"] + 1024] = both.reshape(2, 128, 512).transpose(1, 0, 2).reshape(128, 1024)
    return c.astype(ml_dtypes.bfloat16)


def const_f32():
    c = np.zeros((128, NCF), np.float32)
    c[:, CF["identf"]:CF["identf"] + 128] = np.eye(128)
    t = np.arange(1024)
    rs1 = np.ones((128, 1024), np.float32)
    rsm = np.zeros((128, 1024), np.float32)
    rs1[0:32, t % 256 == 0] = 0.0
    rsm[0:32, t % 256 == 0] = -1e30
    rs1[32:64, t % 256 == 0] = 0.0
    rsm[32:64, t % 256 == 0] = -1e30
    c[:, CF["rs1f"]:CF["rs1f"] + 1024] = rs1
    c[:, CF["rsmf"]:CF["rsmf"] + 1024] = rsm
    dm = np.zeros((128, 16), np.float32)
    for d in range(2):
        for h in range(8):
            dm[32 * d + h, d * 8 + h] = 1.0
    c[:, CF["dmask"]:CF["dmask"] + 16] = dm
    c[:, CF["ones128"]:CF["ones128"] + 128] = 1.0
    return c


def build_program(cfg):
    nc = bass.Bass("TRN2", target_bir_lowering=False)
    nc.dge_precook = False
    plan = unit_plan()
    NU = len(plan)
    uidx = {name: k for k, (name, _) in enumerate(plan)}

    def dram(name, shape, dt, kind="ExternalInput"):
        return nc.dram_tensor(name, shape, dt, kind=kind).ap()

    xT_d = dram("xT", [128, 8, 1024], F32)
    posT_d = dram("posT", [128, 8, 1024], F32)
    condb_d = dram("condb", [128, 1024], F32)
    vecs_d = dram("vecs", [128, NV], F32)
    wmodT_d = dram("wmodT", [4 * 6144, 1024], F32)
    wall_d = dram("wall", [NU * 128, UNITC], F32)
    mixm_d = dram("mixm", [6 * 128, UNITC], BF16)
    cbf_d = dram("cbf", [128, NCB], BF16)
    cf_d = dram("cf", [128, NCF], F32)
    st0_d = dram("st0", [64, 16, 130], F32)
    ngb_d = dram("ngb", [128, 1024], F32)
    yT_d = dram("yT", [128, 8, 1024], F32, kind="ExternalOutput")
    stE_d = dram("stE", [64, 64, 130], F32, kind="ExternalOutput")
    stm_d = dram("stm", [64, 4], F32, kind="ExternalOutput")

    P = Prog(nc, same_engine_sync=cfg.get("same_engine_sync", True))
    xa = P.sb("xa", [128, 8, 1024], F32)
    hT = P.sb("hT", [128, 8, 1024], BF16)
    SCR_B = 90 * 1024
    scr = P.sb("scr", [128, SCR_B // 2], BF16)
    ring = [P.sb("ring%d" % i, [128, UNITC], BF16) for i in range(RING)]
    wm = [P.sb("wm%d" % i, [128, 1024], F32) for i in range(2)]
    vecs = P.sb("vecs", [128, NV], F32)
    cbf = P.sb("cbf", [128, NCB], BF16)
    modr = P.sb("modr", [128, 4 * 48], F32)
    dv = P.sb("dv", [128, 4 * 64], F32)
    junk = P.sb("junk", [128, 1024], BF16)
    junk2 = P.sb("junk2", [128, 128], BF16)
    sb_t = P.sb("sb_t", [128, 1024], F32)
    banks = [P.ps("pb%d" % i, [128, 512], F32) for i in range(8)]

    def sview(byte_off, nbytes, dtype, pattern=None, **kw):
        v = scr[:, byte_off // 2:(byte_off + nbytes) // 2]
        if dtype != BF16:
            v = v.bitcast(dtype)
        if pattern:
            v = v.rearrange(pattern, **kw)
        return v

    def vcol(name, j=0, n=1):
        o = VC[name] + j
        return vecs[:, o:o + n]

    P.dma("sync", lambda e: e.dma_start(out=vecs[:], in_=vecs_d[:, :]), writes=["vecs"])
    P.dma("sync", lambda e: e.dma_start(out=sb_t[:], in_=condb_d[:, :]), writes=["sb_t"])
    P.dma("sync", lambda e: e.dma_start(out=cbf[:], in_=cbf_d[:, :]), writes=["cbf"])
    P.op("scalar", lambda e: e.activation(out=sb_t[:], in_=sb_t[:], func=AF.Silu), reads=["sb_t"], writes=["sb_t"])

    ustate = {"next": 0}

    def want_unit(u, la=RING - 1):
        while ustate["next"] < NU and ustate["next"] <= u + la:
            v = ustate["next"]
            slot = v % RING
            ncols = plan[v][1]
            P.dma("gpsimd", lambda e, v=v, slot=slot, ncols=ncols: e.dma_start(
                out=ring[slot][:, 0:ncols], in_=wall_d[v * 128:(v + 1) * 128, 0:ncols], max_dma_last_dim=4096),
                writes=[("W", slot)])
            ustate["next"] += 1
        return ring[u % RING], ("W", u % RING)

    wmstate = {"n": 0}

    def mod_step(i, c):
        k = wmstate["n"]
        wmstate["n"] += 1
        s = k % 2
        row = (i * 48 + c) * 128
        P.dma("sync", lambda e: e.dma_start(out=wm[s][:], in_=wmodT_d[row:row + 128, :]), writes=[("wm", s)])
        col = i * 48 + c
        P.op("vector", lambda e: e.scalar_tensor_tensor(out=junk[:], in0=wm[s][:], scalar=1.0, in1=sb_t[:],
                                                        op0=ALU.mult, op1=ALU.mult, accum_out=modr[:, col:col + 1]),
             reads=[("wm", s), "sb_t"], writes=[("modr", i, c), "junk"])

    def mod_steps(i):
        for c in range(48):
            yield (i, c)

    def mod_finish_a(i):
        o = i * 48
        b = i * 64
        P.op("vector", lambda e: e.tensor_tensor(out=modr[:, o:o + 16], in0=modr[:, o:o + 16],
                                                 in1=vcol("bmod", o, 16), op=ALU.add),
             reads=[("modr", i, j) for j in range(16)] + ["vecs"], writes=[("modA", i)])
        P.op("vector", lambda e: e.tensor_scalar(out=dv[:, b:b + 8], in0=modr[:, o + 8:o + 16], scalar1=1.0,
                                                 scalar2=1.0 / ALPHA, op0=ALU.add, op1=ALU.mult),
             reads=[("modA", i)], writes=[("dvA", i)])

    def mod_finish_b(i):
        last = (i == DEPTH - 1)
        o = i * 48
        P.op("vector", lambda e: e.tensor_tensor(out=modr[:, o + 16:o + 48], in0=modr[:, o + 16:o + 48],
                                                 in1=vcol("bmod", o + 16, 32), op=ALU.add),
             reads=[("modr", i, j) for j in range(16, 48)] + ["vecs"], writes=[("mod", i)])
        b = i * 64
        P.op("vector", lambda e: e.tensor_scalar(out=dv[:, b + 8:b + 16], in0=modr[:, o + 32:o + 40], scalar1=1.0,
                                                 scalar2=1.0 / ALPHA, op0=ALU.add, op1=ALU.mult),
             reads=[("mod", i)], writes=[("dv", i)])
        for j in range(2):
            sc = 1.0 if (last and j == 1) else ALPHA
            lo = VC["lng"] + (i * 2 + j) * 8
            lb = VC["lnb"] + (i * 2 + j) * 8
            P.op("vector", lambda e, lo=lo, j=j, sc=sc: e.tensor_scalar(out=dv[:, b + 16 + 16 * j:b + 24 + 16 * j], in0=vecs[:, lo:lo + 8],
                                                                       scalar1=sc, scalar2=None, op0=ALU.mult),
                 reads=["vecs", ("dv", i)], writes=[("dv", i)])
            P.op("vector", lambda e, lb=lb, j=j, sc=sc: e.tensor_scalar(out=dv[:, b + 24 + 16 * j:b + 32 + 16 * j], in0=vecs[:, lb:lb + 8],
                                                                       scalar1=sc, scalar2=None, op0=ALU.mult),
                 reads=["vecs", ("dv", i)], writes=[("dv", i)])
        P.op("vector", lambda e: e.tensor_copy(out=junk[:, 0:1], in_=junk[:, 0:1]), reads=[("mod", i), ("dv", i)], writes=["modall", "junk"])

    pend = {"it": iter(()), "layer": None, "done": True}

    def pump(n):
        for _ in range(n):
            st = next(pend["it"], None)
            if st is None:
                break
            mod_step(*st)

    def finish_pending():
        if pend["done"]:
            return
        pump(1000)
        mod_finish_b(pend["layer"])
        pend["done"] = True

    def g1(i, c):
        return modr[:, i * 48 + 16 + c:i * 48 + 17 + c]

    def g2(i, c):
        return modr[:, i * 48 + 40 + c:i * 48 + 41 + c]

    def sh(i, j, c):
        o = i * 48 + (0 if j == 0 else 24) + c
        return modr[:, o:o + 1]

    def mm(i, j, c):
        o = i * 64 + 8 * j + c
        return dv[:, o:o + 1]

    def Ga(i, j, c):
        o = i * 64 + 16 + 16 * j + c
        return dv[:, o:o + 1]

    def Ba(i, j, c):
        o = i * 64 + 24 + 16 * j + c
        return dv[:, o:o + 1]

    bank_rr = {"n": 0}

    def next_bank(pool=(0, 1, 2, 3)):
        b = pool[bank_rr["n"] % len(pool)]
        bank_rr["n"] += 1
        return b

    def modulate(i, j, halves=(0, 1)):
        for hf in halves:
            for c in range(8):
                sl = slice(hf * 512, (hf + 1) * 512)
                P.op("scalar", lambda e, c=c, sl=sl: e.activation(out=hT[:, c, sl], in_=xa[:, c, sl], func=AF.Identity,
                                                                  scale=mm(i, j, c), bias=sh(i, j, c)),
                     reads=[("xa", c, hf)] + ([("dvA", i), ("modA", i)] if j == 0 else [("dv", i), ("mod", i)]), writes=[("H", c, hf)])

    OFF_AT = 0
    OFF_ZB = 64 * 1024
    OFF_ZSQ = 72 * 1024
    OFF_ST = 80 * 1024
    OFF_RT = 86 * 1024
    zb = sview(OFF_ZB, 8192, BF16, "p (c t) -> p c t", c=8)
    zsq = sview(OFF_ZSQ, 8192, BF16, "p (c t) -> p c t", c=8)
    mu = sview(OFF_ST, 2048, F32)
    musq = sview(OFF_ST + 2048, 2048, F32)
    var = sview(OFF_ST + 4096, 2048, F32)
    rtmp = [sview(OFF_RT + k * 2048, 2048, F32) for k in range(2)]
    ones_bf = cbf[:, CB["ones"]:CB["ones"] + 128]
    ident_bf = cbf[:, CB["ident"]:CB["ident"] + 128]

    def ln_phases(i, j, hf, do_h=True, nexti=None, nextj=None):
        sl = slice(hf * 512, (hf + 1) * 512)
        for c in range(8):
            P.op("scalar", lambda e, c=c, sl=sl: e.activation(out=zb[:, c, :], in_=xa[:, c, sl], func=AF.Copy),
                 reads=[("xa", c, hf)], writes=[("S", "zb", c)])
            P.op("scalar", lambda e, c=c, sl=sl: e.activation(out=zsq[:, c, :], in_=xa[:, c, sl], func=AF.Square),
                 reads=[("xa", c, hf)], writes=[("S", "zsq", c)])
        yield 1
        for c in range(8):
            P.op("tensor", lambda e, c=c: e.matmul(banks[4][:], lhsT=ones_bf, rhs=zb[:, c, :], start=(c == 0), stop=(c == 7)),
                 reads=[("S", "zb", c), "cbf"], writes=[("ps", 4)], chain=True)
        for c in range(8):
            P.op("tensor", lambda e, c=c: e.matmul(banks[5][:], lhsT=ones_bf, rhs=zsq[:, c, :], start=(c == 0), stop=(c == 7)),
                 reads=[("S", "zsq", c), "cbf"], writes=[("ps", 5)], chain=True)
        P.op("scalar", lambda e: e.mul(out=mu, in_=banks[4][:], mul=1.0 / D), reads=[("ps", 4)], writes=[("S", "mu")])
        P.op("vector", lambda e: e.tensor_tensor(out=musq, in0=mu, in1=mu, op=ALU.mult), reads=[("S", "mu")], writes=[("S", "musq")])
        P.op("vector", lambda e: e.scalar_tensor_tensor(out=var, in0=banks[5][:], scalar=1.0 / D, in1=musq,
                                                        op0=ALU.mult, op1=ALU.subtract),
             reads=[("ps", 5), ("S", "musq")], writes=[("S", "var")])
        P.op("vector", lambda e: e.tensor_scalar(out=var, in0=var, scalar1=LN_EPS, scalar2=None, op0=ALU.add),
             reads=[("S", "var")], writes=[("S", "var")])
        P.op("scalar", lambda e: e.activation(out=var, in_=var, func=AF.Sqrt), reads=[("S", "var")], writes=[("S", "var")])
        P.op("vector", lambda e: e.reciprocal(out=var, in_=var), reads=[("S", "var")], writes=[("S", "var")])
        yield 2
        for c in range(8):
            P.op("vector", lambda e, c=c, sl=sl: e.tensor_tensor(out=xa[:, c, sl], in0=xa[:, c, sl], in1=mu, op=ALU.subtract),
                 reads=[("xa", c, hf), ("S", "mu")], writes=[("xa", c, hf)])
            P.op("vector", lambda e, c=c, sl=sl: e.tensor_tensor(out=xa[:, c, sl], in0=xa[:, c, sl], in1=var, op=ALU.mult),
                 reads=[("xa", c, hf), ("S", "var")], writes=[("xa", c, hf)])
            P.op("scalar", lambda e, c=c, sl=sl: e.activation(out=xa[:, c, sl], in_=xa[:, c, sl], func=AF.Identity,
                                                              scale=Ga(i, j, c), bias=Ba(i, j, c)),
                 reads=[("xa", c, hf), ("dv", i)], writes=[("xa", c, hf)])
        if do_h:
            modulate(nexti, nextj, halves=(hf,))
        yield 3

    def layer_norm(i, j, do_h=True, nexti=None, nextj=None):
        for hf in range(2):
            for _ in ln_phases(i, j, hf, do_h, nexti, nextj):
                pass

    def epilogue(ps_bank, gvec, c, hf, extra=()):
        sl = slice(hf * 512, (hf + 1) * 512)
        P.op("vector", lambda e: e.scalar_tensor_tensor(out=xa[:, c, sl], in0=banks[ps_bank][:], scalar=gvec, in1=xa[:, c, sl],
                                                        op0=ALU.mult, op1=ALU.add),
             reads=[("ps", ps_bank), ("xa", c, hf), "modall"] + list(extra), writes=[("xa", c, hf)])

    def out_proj(i, unit_name, srcT, src_res, gfun):
        u = uidx[unit_name] if not isinstance(unit_name, int) else unit_name
        rt, rres = want_unit(u)
        w = rt[:, :].rearrange("p (k n) -> p k n", k=8)
        for dc in range(8):
            for hf in range(2):
                b = next_bank()
                for kc in range(8):
                    P.op("tensor", lambda e, b=b, kc=kc, dc=dc, hf=hf: e.matmul(
                        banks[b][:], lhsT=w[:, kc, dc * 128:(dc + 1) * 128], rhs=srcT[:, kc, hf * 512:(hf + 1) * 512],
                        start=(kc == 0), stop=(kc == 7)),
                        reads=[rres] + (src_res(kc, hf) if isinstance(src_res(kc, hf), list) else [src_res(kc, hf)]), writes=[("ps", b)], chain=True)
                epilogue(b, gfun(dc), dc, hf)

    aT = sview(OFF_AT, 65536, BF16, "p (f t) -> p f t", f=32)

    def mlp(i, steps, nxt, g1=iter(())):
        cnt = {"g": 0}

        def w1_group(w, rres, f, fl, hf):
            b = next_bank()
            sl = slice(hf * 512, (hf + 1) * 512)
            for kc in range(8):
                P.op("tensor", lambda e, b=b, kc=kc, fl=fl, sl=sl, w=w: e.matmul(
                    banks[b][:], lhsT=w[:, kc, fl * 128:(fl + 1) * 128], rhs=hT[:, kc, sl],
                    start=(kc == 0), stop=(kc == 7)),
                    reads=[rres, ("H", kc, hf)], writes=[("ps", b)], chain=True)
            r = cnt["g"] % 2
            P.op("scalar", lambda e, b=b, r=r: e.activation(out=rtmp[r], in_=banks[b][:], func=AF.Relu),
                 reads=[("ps", b)], writes=[("S", "rt", r)])
            P.op("scalar", lambda e, r=r, f=f, sl=sl: e.activation(out=aT[:, f, sl], in_=rtmp[r], func=AF.Square),
                 reads=[("S", "rt", r)], writes=[("S", "aT", f, hf)])
            if cnt["g"] % 4 != 3:
                st = next(steps, None)
                if st is not None:
                    mod_step(*st)
            cnt["g"] += 1

        ws = []
        for blk in range(2):
            rt, rres = want_unit(uidx["w1_%d_%d" % (i, blk)], la=(RING - 1 if blk == 0 else 1))
            ws.append((rt[:, :].rearrange("p (k n) -> p k n", k=8), rres))
        for hf in range(2):
            for blk in range(2):
                w, rres = ws[blk]
                for fl in range(8):
                    f = blk * 8 + fl
                    if hf == 0 and f in (1, 4, 7):
                        next(g1, None)
                    w1_group(w, rres, f, fl, hf)
            if hf == 0:
                for _ in g1:
                    pass
        for blk in range(2, 4):
            rt, rres = want_unit(uidx["w1_%d_%d" % (i, blk)])
            w = rt[:, :].rearrange("p (k n) -> p k n", k=8)
            for fl in range(8):
                for hf in range(2):
                    w1_group(w, rres, blk * 8 + fl, fl, hf)
        do_h = nxt is not None
        g = iter(())
        ngrp = 0
        for hf in range(2):
            sl = slice(hf * 512, (hf + 1) * 512)
            for s in range(4):
                rt, rres = want_unit(uidx[("w2_%d_%d" if hf == 0 else "w2b_%d_%d") % (i, s)])
                w = rt[:, :].rearrange("p (f n) -> p f n", f=32)
                for dl in range(2):
                    dc = 2 * s + dl
                    b = next_bank()
                    for f in range(32):
                        P.op("tensor", lambda e, b=b, f=f, dl=dl, sl=sl, w=w: e.matmul(
                            banks[b][:], lhsT=w[:, f, dl * 128:(dl + 1) * 128], rhs=aT[:, f, sl],
                            start=(f == 0), stop=(f == 31)),
                            reads=[rres, ("S", "aT", f, hf)], writes=[("ps", b)], chain=True)
                    epilogue(b, g2(i, dc), dc, hf)
                    for _ in range(3):
                        st = next(steps, None)
                        if st is not None:
                            mod_step(*st)
                    if hf == 1:
                        next(g, None)
            if hf == 0:
                if nxt is not None:
                    mod_finish_a(nxt)
                g = ln_phases(i, 1, 0, do_h, nxt, 0)
        for _ in g:
            pass
        for st in steps:
            mod_step(*st)
        if nxt is not None:
            mod_finish_b(nxt)
        for _ in ln_phases(i, 1, 1, do_h, nxt, 0):
            pass

    def mixer_conv(i):
        OFF_U, OFF_CU, OFF_BC, OFF_C0 = 0, 16384, 32768, 49152
        uT = sview(OFF_U, 16384, BF16, "p (c t) -> p c t", c=8)
        cuT = sview(OFF_CU, 16384, BF16, "p (c t) -> p c t", c=8)
        bcT = sview(OFF_BC, 16384, BF16, "p (c t) -> p c t", c=8)
        c0 = [sview(OFF_C0 + k * 4096, 4096, F32) for k in range(2)]
        for which, uname in ((0, "sc_u"), (1, "sc_cg")):
            rt, rres = want_unit(uidx[uname])
            w = rt[:, :].rearrange("p (k n) -> p k n", k=8)
            for nc_ in range(8):
                for hf in range(2):
                    sl = slice(hf * 512, (hf + 1) * 512)
                    b = next_bank()
                    for kc in range(8):
                        P.op("tensor", lambda e, b=b, kc=kc, nc_=nc_, sl=sl, w=w: e.matmul(
                            banks[b][:], lhsT=w[:, kc, nc_ * 128:(nc_ + 1) * 128], rhs=hT[:, kc, sl],
                            start=(kc == 0), stop=(kc == 7)),
                            reads=[rres, ("H", kc, hf)], writes=[("ps", b)], chain=True)
                    if which == 0:
                        P.op("scalar", lambda e, b=b, nc_=nc_, sl=sl: e.activation(out=uT[:, nc_, sl], in_=banks[b][:], func=AF.Copy),
                             reads=[("ps", b)], writes=[("S", "u", nc_, hf)])
                    else:
                        P.op("vector", lambda e, b=b, nc_=nc_, sl=sl: e.tensor_tensor(out=cuT[:, nc_, sl], in0=banks[b][:], in1=uT[:, nc_, sl], op=ALU.mult),
                             reads=[("ps", b), ("S", "u", nc_, hf)], writes=[("S", "cu", nc_, hf)])
        rt, rres = want_unit(uidx["sc_bg"])
        w = rt[:, :].rearrange("p (k n) -> p k n", k=8)
        for nc_ in range(8):
            cc = c0[nc_ % 2]
            cres = ("S", "c0", nc_ % 2)
            cw = lambda k, nc_=nc_: vecs[:, VC["convw"] + k * 8 + nc_:VC["convw"] + k * 8 + nc_ + 1]
            rd = [("S", "cu", nc_, 0), ("S", "cu", nc_, 1), "vecs"]
            P.op("vector", lambda e, nc_=nc_, cc=cc, cw=cw: e.tensor_scalar(out=cc, in0=cuT[:, nc_, :], scalar1=cw(1), scalar2=None, op0=ALU.mult),
                 reads=rd, writes=[cres])
            P.op("vector", lambda e, nc_=nc_, cc=cc, cw=cw: e.scalar_tensor_tensor(out=cc[:, 1:1024], in0=cuT[:, nc_, 0:1023], scalar=cw(0), in1=cc[:, 1:1024],
                                                                                  op0=ALU.mult, op1=ALU.add),
                 reads=rd + [cres], writes=[cres])
            P.op("vector", lambda e, nc_=nc_, cc=cc, cw=cw: e.scalar_tensor_tensor(out=cc[:, 0:1023], in0=cuT[:, nc_, 1:1024], scalar=cw(2), in1=cc[:, 0:1023],
                                                                                  op0=ALU.mult, op1=ALU.add),
                 reads=rd + [cres], writes=[cres])
            ccv = cc.rearrange("p (s t) -> p s t", s=4)
            cuv = cuT[:, nc_, :].rearrange("p (s t) -> p s t", s=4)
            nb0 = sview(OFF_C0 + 8192 + (nc_ % 2) * 64, 32, F32, "p (s t) -> p s t", s=4)
            P.op("vector", lambda e, cuv=cuv, nb0=nb0, cw=cw: e.tensor_scalar(out=nb0[:, 0:3, 0:1], in0=cuv[:, 0:3, 255:256], scalar1=cw(0), scalar2=vcol("bflag"),
                                                                             op0=ALU.mult, op1=ALU.mult),
                 reads=rd, writes=[("S", "nb", nc_ % 2)])
            P.op("vector", lambda e, cuv=cuv, nb0=nb0, cw=cw: e.tensor_scalar(out=nb0[:, 0:3, 1:2], in0=cuv[:, 1:4, 0:1], scalar1=cw(2), scalar2=vcol("bflag"),
                                                                             op0=ALU.mult, op1=ALU.mult),
                 reads=rd + [("S", "nb", nc_ % 2)], writes=[("S", "nb", nc_ % 2)])
            P.op("vector", lambda e, ccv=ccv, nb0=nb0: e.tensor_tensor(out=ccv[:, 1:4, 0:1], in0=ccv[:, 1:4, 0:1], in1=nb0[:, 0:3, 0:1], op=ALU.subtract),
                 reads=[cres, ("S", "nb", nc_ % 2)], writes=[cres])
            P.op("vector", lambda e, ccv=ccv, nb0=nb0: e.tensor_tensor(out=ccv[:, 0:3, 255:256], in0=ccv[:, 0:3, 255:256], in1=nb0[:, 0:3, 1:2], op=ALU.subtract),
                 reads=[cres, ("S", "nb", nc_ % 2)], writes=[cres])
            for hf in range(2):
                sl = slice(hf * 512, (hf + 1) * 512)
                b = next_bank()
                for kc in range(8):
                    P.op("tensor", lambda e, b=b, kc=kc, nc_=nc_, sl=sl, w=w: e.matmul(
                        banks[b][:], lhsT=w[:, kc, nc_ * 128:(nc_ + 1) * 128], rhs=hT[:, kc, sl],
                        start=(kc == 0), stop=(kc == 7)),
                        reads=[rres, ("H", kc, hf)], writes=[("ps", b)], chain=True)
                P.op("vector", lambda e, b=b, nc_=nc_, sl=sl, cc=cc: e.tensor_tensor(out=bcT[:, nc_, sl], in0=banks[b][:], in1=cc[:, sl], op=ALU.mult),
                     reads=[("ps", b), cres], writes=[("S", "bc", nc_, hf)])
        out_proj(i, "sc_out", bcT, lambda kc, hf: ("S", "bc", kc, hf), lambda dc: g1(i, dc))

    def load_mix(k, slot_hint):
        raise NotImplementedError

    def mixer_pool(i):
        OFF_HT, OFF_PT, OFF_MM = 0, 16384, 32768
        htok = sview(OFF_HT, 16384, BF16, "p (t d) -> p t d", t=8)
        pT = sview(OFF_PT, 16384, BF16, "p (c t) -> p c t", c=8)
        mbuf = [sview(OFF_MM + k * 16384, 16384, BF16, "p (s t) -> p s t", s=8) for k in range(2)]
        gsb = sview(OFF_MM + 32768, 64, F32)
        P.op("vector", lambda e: e.tensor_tensor(out=gsb[:, 0:8], in0=modr[:, i * 48 + 16:i * 48 + 24], in1=vcol("pls", 0, 8), op=ALU.mult),
             reads=["modall", "vecs"], writes=[("S", "gsb")])
        P.op("vector", lambda e: e.tensor_tensor(out=gsb[:, 8:16], in0=gsb[:, 0:8], in1=vcol("plb", 0, 8), op=ALU.mult),
             reads=[("S", "gsb"), "vecs"], writes=[("S", "gsb")])
        for tt in range(8):
            b = next_bank((6, 7))
            pv = banks[b][:].bitcast(BF16)
            for c in range(8):
                P.op("tensor", lambda e, pv=pv, c=c, tt=tt: e.transpose(out=pv[:, c * 128:(c + 1) * 128], in_=hT[:, c, tt * 128:(tt + 1) * 128], identity=ident_bf),
                     reads=[("H", c, tt // 4), "cbf"], writes=[("ps", b)], chain=(c > 0))
            P.op("scalar", lambda e, pv=pv, tt=tt: e.activation(out=htok[:, tt, :], in_=pv, func=AF.Copy),
                 reads=[("ps", b)], writes=[("S", "htok", tt)])
        for c in range(8):
            for hf in range(2):
                sl = slice(hf * 512, (hf + 1) * 512)
                P.op("vector", lambda e, c=c, sl=sl: e.tensor_scalar(out=xa[:, c, sl], in0=xa[:, c, sl], scalar1=gsb[:, 8 + c:9 + c], scalar2=None, op0=ALU.add),
                     reads=[("xa", c, hf), ("S", "gsb")], writes=[("xa", c, hf)])
        for g in range(4):
            mb = mbuf[g % 2]
            P.dma("sync", lambda e, g=g, mb=mb: e.dma_start(out=mb, in_=mixm_d[g * 128:(g + 1) * 128, :].rearrange("p (s t) -> p s t", s=8)),
                  writes=[("S", "mb", g % 2)])
            for cl in range(2):
                c = 2 * g + cl
                for hf in range(2):
                    sl = slice(hf * 512, (hf + 1) * 512)
                    b = next_bank()
                    for sc in range(8):
                        P.op("tensor", lambda e, b=b, sc=sc, c=c, sl=sl, mb=mb: e.matmul(
                            banks[b][:], lhsT=htok[:, sc, c * 128:(c + 1) * 128], rhs=mb[:, sc, sl], start=(sc == 0), stop=(sc == 7)),
                            reads=[("S", "htok", sc), ("S", "mb", g % 2)], writes=[("ps", b)], chain=True)
                    P.op("scalar", lambda e, b=b, c=c, sl=sl: e.activation(out=pT[:, c, sl], in_=banks[b][:], func=AF.Copy),
                         reads=[("ps", b)], writes=[("S", "pT", c, hf)])
        rt, rres = want_unit(uidx["pl_w"])
        w = rt[:, 0:2048].rearrange("p (g c n) -> p g c n", g=4, c=2)
        for g in range(4):
            for dl in range(2):
                dc = 2 * g + dl
                for hf in range(2):
                    sl = slice(hf * 512, (hf + 1) * 512)
                    b = next_bank()
                    for cl in range(2):
                        P.op("tensor", lambda e, b=b, g=g, cl=cl, dl=dl, sl=sl: e.matmul(
                            banks[b][:], lhsT=w[:, g, cl, dl * 128:(dl + 1) * 128], rhs=pT[:, 2 * g + cl, sl], start=(cl == 0), stop=(cl == 1)),
                            reads=[rres, ("S", "pT", 2 * g + cl, hf)], writes=[("ps", b)], chain=True)
                    epilogue(b, gsb[:, dc:dc + 1], dc, hf, extra=[("S", "gsb")])

    def mixer_fourier(i):
        OFF_P1, OFF_P2, OFF_FT, OFF_M = 0, 16384, 32768, 49152
        P1 = sview(OFF_P1, 16384, BF16, "p (t n) -> p t n", t=8)
        P2 = sview(OFF_P2, 16384, BF16, "p (t n) -> p t n", t=8)
        FT = sview(OFF_FT, 16384, BF16, "p (c t) -> p c t", c=8)
        mb = [sview(OFF_M + k * 16384, 16384, BF16, "p (s t) -> p s t", s=8) for k in range(2)]
        gb = sview(OFF_M + 32768, 32, F32)
        cwsw = cbf[:, CB["cwsw"]:CB["cwsw"] + 1024].rearrange("p (c n) -> p c n", c=2)
        for k in range(2):
            P.dma("sync", lambda e, k=k: e.dma_start(out=mb[k], in_=mixm_d[(4 + k) * 128:(5 + k) * 128, :].rearrange("p (s t) -> p s t", s=8)),
                  writes=[("S", "fm", k)])
        P.op("vector", lambda e: e.tensor_tensor(out=gb[:, 0:8], in0=modr[:, i * 48 + 16:i * 48 + 24], in1=vcol("ftb", 0, 8), op=ALU.mult),
             reads=["modall", "vecs"], writes=[("S", "gb")])
        for c in range(8):
            for hf in range(2):
                sl = slice(hf * 512, (hf + 1) * 512)
                P.op("vector", lambda e, c=c, sl=sl: e.tensor_scalar(out=xa[:, c, sl], in0=xa[:, c, sl], scalar1=gb[:, c:c + 1], scalar2=None, op0=ALU.add),
                     reads=[("xa", c, hf), ("S", "gb")], writes=[("xa", c, hf)])
        for tt in range(8):
            for g in range(4):
                b = next_bank()
                for cl in range(2):
                    P.op("tensor", lambda e, b=b, g=g, cl=cl, tt=tt: e.matmul(
                        banks[b][:], lhsT=hT[:, 2 * g + cl, tt * 128:(tt + 1) * 128], rhs=cwsw[:, cl, :], start=(cl == 0), stop=(cl == 1)),
                        reads=[("H", 2 * g + cl, tt // 4), "cbf"], writes=[("ps", b)], chain=True)
                P.op("scalar", lambda e, b=b, g=g, tt=tt: e.activation(out=P1[:, tt, g * 256:(g + 1) * 256], in_=banks[b][:, 0:256], func=AF.Copy),
                     reads=[("ps", b)], writes=[("S", "P1", tt, g)])
                P.op("vector", lambda e, b=b, g=g, tt=tt: e.tensor_copy(out=P2[:, tt, g * 256:(g + 1) * 256], in_=banks[b][:, 256:512]),
                     reads=[("ps", b)], writes=[("S", "P2", tt, g), ("ps", b)])
        for c in range(8):
            g = c // 2
            for hf in range(2):
                sl = slice(hf * 512, (hf + 1) * 512)
                b = next_bank()
                n = 0
                for k, Pk, nm in ((0, P1, "P1"), (1, P2, "P2")):
                    for tt in range(8):
                        P.op("tensor", lambda e, b=b, k=k, Pk=Pk, tt=tt, c=c, sl=sl, n=n: e.matmul(
                            banks[b][:], lhsT=Pk[:, tt, c * 128:(c + 1) * 128], rhs=mb[k][:, tt, sl], start=(n == 0), stop=(n == 15)),
                            reads=[("S", nm, tt, g), ("S", "fm", k)], writes=[("ps", b)], chain=True)
                        n += 1
                P.op("scalar", lambda e, b=b, c=c, sl=sl: e.activation(out=FT[:, c, sl], in_=banks[b][:], func=AF.Copy),
                     reads=[("ps", b)], writes=[("S", "FT", c, hf)])
        out_proj(i, "ft_out", FT, lambda kc, hf: ("S", "FT", kc, hf), lambda dc: g1(i, dc))

    def mixer_mlstm(i):
        OV, OAK, OCF, OAB, OSM, OCST, OST, OEF = 0, 16640, 22784, 23872, 24128, 25152, 27232, 28768
        OHF, OSEG, OHS, OTMP, OHN = 29824, 46208, 62592, 70784, 74880
        Vx = sview(OV, 16640, BF16, "p (t h n) -> p t h n", t=8, h=8)
        akT = sview(OAK, 6144, F32, "p (t q r) -> p t q r", t=8, q=3)
        identf = sview(OCF, 512, F32)
        dmask = sview(OCF + 512, 64, F32)
        ones128 = sview(OCF + 576, 512, F32)
        aendb = sview(OAB, 256, F32)
        sm = sview(OSM, 1024, F32)
        mst, nmst, mend, aend = sm[:, 0:4], sm[:, 4:8], sm[:, 8:12], sm[:, 12:16]
        Xd = sm[:, 16:80]
        s1, s2, mean_, rstd_ = sm[:, 80:88], sm[:, 88:96], sm[:, 96:104], sm[:, 104:112]
        dn = [sm[:, 112 + 2 * k:114 + 2 * k] for k in range(2)]
        Cst = sview(OCST, 2080, BF16, "p (d q n) -> p d q n", d=2, q=4)
        St = [sview(OST + k * 768, 768, BF16, "p (b t) -> p b t", b=3) for k in range(2)]
        Ef = [sview(OEF + k * 520, 520, F32) for k in range(2)]
        hf = sview(OHF, 16384, BF16, "p (t n) -> p t n", t=8)
        qkt = [sview(OSEG + k * 2048, 2048, BF16) for k in range(4)]
        qTs = [sview(OSEG + 8192 + k * 2048, 2048, BF16, "p (q t) -> p q t", q=4) for k in range(2)]
        kTs = [sview(OSEG + 12288 + k * 2048, 2048, BF16, "p (q t) -> p q t", q=4) for k in range(2)]
        hs = sview(OHS, 8192, F32, "p (j n) -> p j n", j=2)
        sg = sview(OTMP, 2048, F32)
        hgt = sview(OTMP + 2048, 2048, BF16)
        hnT = sview(OHN, 16384, BF16, "p (c t) -> p c t", c=8)
        Cf32 = sview(OHN, 4160, F32, "p (d q n) -> p d q n", d=2, q=4)
        gt = [sview(OHF + k * 4096, 4096, F32) for k in range(10)]
        gi, lf, bb, uu, cm, al, ka, ep, rs1, rsm = gt
        R = slice(0, 64)
        maskF = cbf[:, CB["maskF"]:CB["maskF"] + 128]
        maskB = cbf[:, CB["maskB"]:CB["maskB"] + 128]
        S_ = lambda *k: ("S",) + k

        P.dma("sync", lambda e: e.dma_start(out=identf, in_=cf_d[:, CF["identf"]:CF["identf"] + 128]), writes=[S_("identf")])
        P.dma("sync", lambda e: e.dma_start(out=dmask, in_=cf_d[:, CF["dmask"]:CF["dmask"] + 16]), writes=[S_("dmask")])
        P.dma("sync", lambda e: e.dma_start(out=ones128, in_=cf_d[:, CF["ones128"]:CF["ones128"] + 128]), writes=[S_("ones128")])
        P.dma("sync", lambda e: e.dma_start(out=rs1, in_=cf_d[:, CF["rs1f"]:CF["rs1f"] + 1024]), writes=[S_("rs1")])
        P.dma("sync", lambda e: e.dma_start(out=rsm, in_=cf_d[:, CF["rsmf"]:CF["rsmf"] + 1024]), writes=[S_("rsm")])
        st0v = st0_d.rearrange("p (d q par) n -> p d q par n", d=2, q=4)
        for par in range(2):
            P.dma("sync", lambda e, par=par: e.dma_start(out=Cf32[64 * par:64 * par + 64, :, :, :], in_=st0v[:, :, :, par, :]),
                  writes=[S_("Cf32", par)])
        P.op("scalar", lambda e: e.activation(out=Cst, in_=Cf32, func=AF.Copy), reads=[S_("Cf32", 0), S_("Cf32", 1)], writes=[S_("Cst", d_, h_) for d_ in range(2) for h_ in range(8)])
        P.op("vector", lambda e: e.memset(Vx[:, :, :, 128:130], 1.0), writes=[S_("Vones")])

        rt, rres = want_unit(uidx["ml_gate"])
        wg = rt[:, 0:1024].rearrange("p (k n) -> p k n", k=8)
        for hh in range(2):
            sl = slice(hh * 512, (hh + 1) * 512)
            for which in range(2):
                b = next_bank()
                for kc in range(8):
                    P.op("tensor", lambda e, b=b, kc=kc, sl=sl, which=which: e.matmul(
                        banks[b][0:64, :], lhsT=wg[:, kc, which * 64:(which + 1) * 64], rhs=hT[:, kc, sl], start=(kc == 0), stop=(kc == 7)),
                        reads=[rres, ("H", kc, hh)], writes=[("ps", b)], chain=True)
                if which == 0:
                    P.op("scalar", lambda e, b=b, sl=sl: e.activation(out=gi[R, sl], in_=banks[b][0:64, :], func=AF.Identity, bias=vecs[R, VC["gbi"]:VC["gbi"] + 1]),
                         reads=[("ps", b), "vecs"], writes=[S_("gi", hh)])
                else:
                    P.op("scalar", lambda e, b=b, sl=sl: e.activation(out=lf[R, sl], in_=banks[b][0:64, :], func=AF.Exp, scale=-1.0, bias=vecs[R, VC["ngbf"]:VC["ngbf"] + 1]),
                         reads=[("ps", b), "vecs"], writes=[S_("lf", hh)])
        P.op("scalar", lambda e: e.activation(out=lf[R, :], in_=lf[R, :], func=AF.Ln, bias=1.0), reads=[S_("lf", 0), S_("lf", 1)], writes=[S_("lf")])
        P.op("scalar", lambda e: e.mul(out=lf[R, :], in_=lf[R, :], mul=-1.0), reads=[S_("lf")], writes=[S_("lf")])
        P.op("vector", lambda e: e.tensor_tensor_scan(out=bb[R, :], data0=rs1[R, :], data1=lf[R, :], initial=0.0, op0=ALU.mult, op1=ALU.add),
             reads=[S_("lf"), S_("rs1")], writes=[S_("bb", 0), S_("bb", 1)])
        RB = slice(32, 64)
        P.op("vector", lambda e: e.tensor_tensor(out=uu[RB, :], in0=lf[RB, :], in1=bb[RB, :], op=ALU.subtract),
             reads=[S_("lf"), S_("bb", 1)], writes=[S_("uu")])
        for s in range(4):
            seg = slice(s * 256, (s + 1) * 256)
            P.op("vector", lambda e, s=s, seg=seg: e.tensor_scalar(out=uu[RB, seg], in0=uu[RB, seg], scalar1=bb[RB, s * 256 + 255:s * 256 + 256], scalar2=None, op0=ALU.add),
                 reads=[S_("uu"), S_("bb", 1)], writes=[S_("uu")])
        P.op("vector", lambda e: e.tensor_copy(out=bb[RB, :], in_=uu[RB, :]), reads=[S_("uu")], writes=[S_("bb", 1)])
        P.op("vector", lambda e: e.tensor_tensor(out=uu[R, :], in0=gi[R, :], in1=bb[R, :], op=ALU.subtract),
             reads=[S_("gi", 0), S_("gi", 1), S_("bb", 0), S_("bb", 1)], writes=[S_("uu")])
        P.op("vector", lambda e: e.tensor_tensor_scan(out=cm[0:32, :], data0=rsm[0:32, :], data1=uu[0:32, :], initial=-1e30, op0=ALU.add, op1=ALU.max),
             reads=[S_("uu"), S_("rsm")], writes=[S_("cm", 0)])
        v3 = lambda t_: t_.rearrange("p (s t) -> p s t", s=4)
        src = uu
        for k in range(8):
            shf = 1 << k
            dst = ep if k % 2 == 0 else cm
            sv, dv_ = v3(src), v3(dst)
            P.op("vector", lambda e, sv=sv, dv_=dv_, shf=shf: e.tensor_tensor(out=dv_[RB, :, 0:256 - shf], in0=sv[RB, :, 0:256 - shf], in1=sv[RB, :, shf:256], op=ALU.max),
                 reads=[S_("uu"), S_("hs_pp", k)], writes=[S_("hs_pp", k + 1)])
            P.op("vector", lambda e, sv=sv, dv_=dv_, shf=shf: e.tensor_copy(out=dv_[RB, :, 256 - shf:256], in_=sv[RB, :, 256 - shf:256]),
                 reads=[S_("uu"), S_("hs_pp", k), S_("hs_pp", k + 1)], writes=[S_("hs_pp", k + 1)])
            src = dst
        P.op("vector", lambda e: e.tensor_copy(out=junk[:, 0:1], in_=junk[:, 0:1]), reads=[S_("hs_pp", 8)], writes=[S_("cm", 1), "junk"])
        for d in range(2):
            Rd = slice(32 * d, 32 * d + 32)
            order = [0, 1, 2, 3] if d == 0 else [3, 2, 1, 0]
            for n_, s in enumerate(order):
                seg = slice(s * 256, (s + 1) * 256)
                endc = s * 256 + 255 if d == 0 else s * 256
                if n_ == 0:
                    P.op("vector", lambda e, Rd=Rd, s=s: e.tensor_copy(out=mst[Rd, s:s + 1], in_=vecs[Rd, VC["m0"]:VC["m0"] + 1]),
                         reads=["vecs"], writes=[S_("mst", d)])
                else:
                    ps_ = order[n_ - 1]
                    P.op("vector", lambda e, Rd=Rd, s=s, ps_=ps_: e.tensor_tensor(out=mst[Rd, s:s + 1], in0=mend[Rd, ps_:ps_ + 1], in1=vecs[Rd, VC["flag"]:VC["flag"] + 1], op=ALU.mult),
                         reads=["vecs", S_("mend", d)], writes=[S_("mst", d)])
                P.op("vector", lambda e, Rd=Rd, s=s, seg=seg: e.tensor_scalar(out=cm[Rd, seg], in0=cm[Rd, seg], scalar1=mst[Rd, s:s + 1], scalar2=None, op0=ALU.max),
                     reads=[S_("cm", d), S_("mst", d)], writes=[S_("cm", d)])
                P.op("vector", lambda e, Rd=Rd, s=s, endc=endc: e.tensor_tensor(out=mend[Rd, s:s + 1], in0=bb[Rd, endc:endc + 1], in1=cm[Rd, endc:endc + 1], op=ALU.add),
                     reads=[S_("cm", d), S_("bb", d)], writes=[S_("mend", d)])
        P.op("vector", lambda e: e.tensor_scalar(out=nmst[R, :], in0=mst[R, :], scalar1=-1.0, scalar2=-2.0794415416798357, op0=ALU.mult, op1=ALU.add),
             reads=[S_("mst", 0), S_("mst", 1)], writes=[S_("nmst")])
        P.dma("sync", lambda e: e.dma_start(out=stm_d[:, :], in_=mend[R, :]), reads=[S_("mend", 0), S_("mend", 1)], final=True)
        gread = [S_("cm", 0), S_("cm", 1), S_("mst", 0), S_("mst", 1)]
        for s in range(4):
            seg = slice(s * 256, (s + 1) * 256)
            P.op("scalar", lambda e, s=s, seg=seg: e.activation(out=al[R, seg], in_=cm[R, seg], func=AF.Exp, scale=-1.0, bias=mst[R, s:s + 1]),
                 reads=gread, writes=[S_("al", s)])
            P.op("scalar", lambda e, s=s, seg=seg: e.activation(out=ka[R, seg], in_=uu[R, seg], func=AF.Exp, scale=1.0, bias=nmst[R, s:s + 1]),
                 reads=[S_("uu"), S_("nmst")], writes=[S_("ka", s)])
        P.op("vector", lambda e: e.tensor_tensor(out=ep[R, :], in0=bb[R, :], in1=cm[R, :], op=ALU.add),
             reads=gread + [S_("bb", 0), S_("bb", 1)], writes=[S_("ep")])
        P.op("scalar", lambda e: e.activation(out=ep[R, :], in_=ep[R, :], func=AF.Exp, scale=-1.0), reads=[S_("ep")], writes=[S_("ep")])
        alv = al.rearrange("p (s t) -> p s t", s=4)
        alr = [S_("al", s) for s in range(4)]
        P.op("vector", lambda e: e.tensor_copy(out=aend[0:32, :].unsqueeze(2), in_=alv[0:32, :, 255:256]), reads=alr, writes=[S_("aend", 0)])
        P.op("vector", lambda e: e.tensor_copy(out=aend[32:64, :].unsqueeze(2), in_=alv[32:64, :, 0:1]), reads=alr, writes=[S_("aend", 1)])
        Xv = Xd.rearrange("p (s n) -> p s n", s=4)
        P.op("vector", lambda e: e.tensor_tensor(out=Xv[R, :, :], in0=dmask[R, :].unsqueeze(1).to_broadcast([64, 4, 16]),
                                                 in1=aend[R, :].unsqueeze(2).to_broadcast([64, 4, 16]), op=ALU.mult),
             reads=[S_("aend", 0), S_("aend", 1), S_("dmask")], writes=[S_("Xd")])
        P.op("tensor", lambda e: e.matmul(banks[6][:, 0:64], lhsT=ones128[R, :], rhs=Xd[R, :], start=True, stop=True),
             reads=[S_("Xd"), S_("ones128")], writes=[("ps", 6)])
        P.op("vector", lambda e: e.tensor_copy(out=aendb, in_=banks[6][:, 0:64]), reads=[("ps", 6)], writes=[S_("aendb")])
        for tt in range(8):
            b = next_bank((6, 7))
            tsl = slice(tt * 128, (tt + 1) * 128)
            for q, tile_, nm in ((0, al, "al"), (1, ka, "ka"), (2, ep, "ep")):
                rd = [S_(nm, tt // 2)] if nm != "ep" else [S_("ep")]
                P.op("tensor", lambda e, b=b, q=q, tile_=tile_, tsl=tsl: e.transpose(out=banks[b][:, q * 64:(q + 1) * 64], in_=tile_[R, tsl], identity=identf[R, 0:64]),
                     reads=rd + [S_("identf")], writes=[("ps", b)])
            P.op("scalar", lambda e, b=b, tt=tt: e.activation(out=akT[:, tt, :, :], in_=banks[b][:, 0:192].rearrange("p (q r) -> p q r", q=3), func=AF.Copy),
                 reads=[("ps", b)], writes=[S_("akT", tt)])
        rt, rres = want_unit(uidx["ml_v"])
        wv = rt[:, :].rearrange("p (k n) -> p k n", k=8)
        for tt in range(8):
            tsl = slice(tt * 128, (tt + 1) * 128)
            for vh in range(2):
                b = next_bank()
                for kc in range(8):
                    P.op("tensor", lambda e, b=b, kc=kc, tsl=tsl, vh=vh: e.matmul(
                        banks[b][:], lhsT=hT[:, kc, tsl], rhs=wv[:, kc, vh * 512:(vh + 1) * 512], start=(kc == 0), stop=(kc == 7)),
                        reads=[rres, ("H", kc, tt // 4)], writes=[("ps", b)], chain=True)
                P.op("scalar", lambda e, b=b, tt=tt, vh=vh: e.activation(out=Vx[:, tt, 4 * vh:4 * vh + 4, 0:128], in_=banks[b][:].rearrange("p (h n) -> p h n", h=4), func=AF.Copy),
                     reads=[("ps", b)], writes=[S_("Vx", tt, vh)])
                pump(2)
        finish_pending()
        P.fence("S")

        seq = [(0, sg_) for sg_ in (0, 1, 2, 3)] + [(1, sg_) for sg_ in (3, 2, 1, 0)]
        ust = {}

        def emit_proj(idx):
            d, seg = seq[idx]
            sb_ = idx % 2
            if idx == 0 or idx == 4:
                rt, rres = want_unit(uidx["ml_qk"] if d == 0 else uidx["ml_qk2"])
                ust["wqk"], ust["rres"] = rt[:, :].rearrange("p (k n) -> p k n", k=8), rres
                if d == 1:
                    rto, rreso = want_unit(uidx["ml_o"], la=0)
                    ust["wo"], ust["rreso"] = rto[:, :].rearrange("p (k n) -> p k n", k=8), rreso
            wqk, rres = ust["wqk"], ust["rres"]
            qT, kT = qTs[sb_], kTs[sb_]
            for j in range(2):
                tt = 2 * seg + j
                tsl = slice(tt * 128, (tt + 1) * 128)
                qk = qkt[2 * sb_ + j]
                for part in range(2):
                    b = next_bank()
                    for kc in range(8):
                        P.op("tensor", lambda e, b=b, kc=kc, tsl=tsl, part=part, wqk=wqk: e.matmul(
                            banks[b][:], lhsT=hT[:, kc, tsl], rhs=wqk[:, kc, part * 512:(part + 1) * 512], start=(kc == 0), stop=(kc == 7)),
                            reads=[rres, ("H", kc, tt // 4)], writes=[("ps", b)], chain=True)
                    P.op("vector", lambda e, b=b, qk=qk, part=part, tt=tt, d=d: e.tensor_tensor(
                        out=qk[:, part * 512:(part + 1) * 512].rearrange("p (h n) -> p h n", h=8),
                        in0=banks[b][:].rearrange("p (h n) -> p h n", h=8),
                        in1=akT[:, tt, part, 32 * d:32 * d + 8].unsqueeze(2).to_broadcast([128, 8, 64]), op=ALU.mult),
                        reads=[("ps", b), S_("akT", tt)], writes=[S_("qkt", 2 * sb_ + j, part)])
            for part, dst, nm in ((0, qT, "qT"), (1, kT, "kT")):
                b = next_bank((6, 7))
                pv = banks[b][:].bitcast(BF16).rearrange("p (q t) -> p q t", q=4)
                for pair in range(4):
                    for j in range(2):
                        qk = qkt[2 * sb_ + j]
                        P.op("tensor", lambda e, pv=pv, pair=pair, j=j, qk=qk, part=part: e.transpose(
                            out=pv[:, pair, j * 128:(j + 1) * 128], in_=qk[:, part * 512 + pair * 128:part * 512 + (pair + 1) * 128], identity=ident_bf),
                            reads=[S_("qkt", 2 * sb_ + j, part), "cbf"], writes=[("ps", b)], chain=(pair + j > 0))
                P.op("scalar", lambda e, pv=pv, dst=dst: e.activation(out=dst, in_=pv, func=AF.Copy),
                     reads=[("ps", b)], writes=[S_(nm, sb_)])

        emit_proj(0)
        for idx in range(8):
            if True:
                d, seg = seq[idx]
                sb_ = idx % 2
                qT, kT = qTs[sb_], kTs[sb_]
                mask = maskF if d == 0 else maskB
                if d == 1:
                    wo, rreso = ust["wo"], ust["rreso"]
                if d == 0:
                    blocks = [(0, 0), (0, 1), (1, 1)]
                else:
                    blocks = [(0, 0), (1, 0), (1, 1)]

                def emit_S(h, kT=kT, qT=qT, blocks=blocks, mask=mask, sb_=sb_):
                    pair, pb = h // 2, 64 * (h % 2)
                    PB = slice(pb, pb + 64)
                    kTh, qTh = kT[PB, pair, :], qT[PB, pair, :]
                    sbk = next_bank((0, 1))
                    Sb = banks[sbk]
                    for k_, (ci, cj) in enumerate(blocks):
                        P.op("tensor", lambda e, Sb=Sb, k_=k_, ci=ci, cj=cj, kTh=kTh, qTh=qTh: e.matmul(
                            Sb[:, k_ * 128:(k_ + 1) * 128], lhsT=kTh[:, ci * 128:(ci + 1) * 128], rhs=qTh[:, cj * 128:(cj + 1) * 128], start=True, stop=True),
                            reads=[S_("kT", sb_), S_("qT", sb_)], writes=[("ps", sbk)], chain=(k_ > 0))
                    st = St[h % 2]
                    Sv = Sb[:, 0:384].rearrange("p (b t) -> p b t", b=3)
                    P.op("vector", lambda e, st=st, Sv=Sv, mask=mask: e.tensor_tensor(out=st[:, 0:3:2, :], in0=Sv[:, 0:3:2, :],
                                                                                    in1=mask.unsqueeze(1).to_broadcast([128, 2, 128]), op=ALU.mult),
                         reads=[("ps", sbk), "cbf"], writes=[S_("St", h % 2, 0)])
                    P.op("scalar", lambda e, st=st, Sv=Sv: e.activation(out=st[:, 1, :], in_=Sv[:, 1, :], func=AF.Copy),
                         reads=[("ps", sbk)], writes=[S_("St", h % 2, 1), ("ps", sbk)])

                emit_S(0)
                if idx + 1 < 8:
                    emit_proj(idx + 1)
                for h in range(8):
                    pair, pb = h // 2, 64 * (h % 2)
                    PB = slice(pb, pb + 64)
                    kTh, qTh = kT[PB, pair, :], qT[PB, pair, :]
                    st = St[h % 2]
                    if h < 7:
                        emit_S(h + 1)
                    nbk = next_bank((2, 3))
                    Nb = banks[nbk]
                    strd = [S_("St", h % 2, 0), S_("St", h % 2, 1)]
                    for j in range(2):
                        terms = [(k_, ci) for k_, (ci, cj) in enumerate(blocks) if cj == j]
                        for n_, (k_, ci) in enumerate(terms):
                            P.op("tensor", lambda e, Nb=Nb, j=j, k_=k_, ci=ci, st=st, h=h, n_=n_, seg=seg: e.matmul(
                                Nb[:, j * 130:(j + 1) * 130], lhsT=st[:, k_, :], rhs=Vx[:, 2 * seg + ci, h, :], start=(n_ == 0), stop=False),
                                reads=strd + [S_("Vx", 2 * seg + ci, h // 4), S_("Vones")], writes=[("ps", nbk)], chain=(n_ > 0))
                        P.op("tensor", lambda e, Nb=Nb, j=j, qTh=qTh, PB=PB, pair=pair, d=d: e.matmul(
                            Nb[:, j * 130:(j + 1) * 130], lhsT=qTh[:, j * 128:(j + 1) * 128], rhs=Cst[PB, d, pair, :], start=False, stop=True),
                            reads=[S_("qT", sb_), S_("Cst", d, h)], writes=[("ps", nbk)], chain=True)
                    ebk = next_bank((4, 5))
                    Eb = banks[ebk]
                    for j in range(2):
                        qk = qkt[2 * sb_ + j]
                        P.op("tensor", lambda e, Eb=Eb, PB=PB, qk=qk, h=h, j=j, seg=seg: e.matmul(
                            Eb[PB, 0:130], lhsT=qk[:, 512 + h * 64:512 + (h + 1) * 64], rhs=Vx[:, 2 * seg + j, h, :], start=(j == 0), stop=False),
                            reads=[S_("qkt", 2 * sb_ + j, 1), S_("Vx", 2 * seg + j, h // 4), S_("Vones")], writes=[("ps", ebk)], chain=(j > 0))
                    P.op("tensor", lambda e, Eb=Eb, PB=PB, pair=pair, d=d: e.matmul(
                        Eb[PB, 0:130], lhsT=ident_bf[PB, pb:pb + 64], rhs=Cst[PB, d, pair, :], start=False, stop=True),
                        reads=[S_("Cst", d, h), "cbf"], writes=[("ps", ebk)], chain=True)
                    Nv = Nb[:, 0:260].rearrange("p (j n) -> p j n", j=2)
                    r_ = 32 * d + h
                    dnk = dn[h % 2]
                    P.op("scalar", lambda e, dnk=dnk, Nv=Nv: e.activation(out=dnk.unsqueeze(2), in_=Nv[:, :, 128:129], func=AF.Abs),
                         reads=[("ps", nbk)], writes=[S_("dn", h % 2)])
                    P.op("vector", lambda e, dnk=dnk, seg=seg, r_=r_: e.tensor_tensor(out=dnk.unsqueeze(2), in0=dnk.unsqueeze(2), in1=akT[:, 2 * seg:2 * seg + 2, 2, r_:r_ + 1], op=ALU.max),
                         reads=[S_("dn", h % 2), S_("akT", 2 * seg), S_("akT", 2 * seg + 1)], writes=[S_("dn", h % 2)])
                    P.op("vector", lambda e, dnk=dnk: e.reciprocal(out=dnk, in_=dnk), reads=[S_("dn", h % 2)], writes=[S_("dn", h % 2)])
                    for j in range(2):
                        tt = 2 * seg + j
                        hsl = slice(h * 128, (h + 1) * 128)
                        if d == 0:
                            P.op("scalar", lambda e, Nv=Nv, j=j, tt=tt, hsl=hsl, dnk=dnk: e.activation(out=hf[:, tt, hsl], in_=Nv[:, j, 0:128], func=AF.Copy, scale=dnk[:, j:j + 1]),
                                 reads=[("ps", nbk), S_("dn", h % 2)], writes=[S_("hf", tt, h)])
                        else:
                            P.op("vector", lambda e, Nv=Nv, j=j, tt=tt, hsl=hsl, dnk=dnk: e.scalar_tensor_tensor(out=hs[:, j, hsl], in0=Nv[:, j, 0:128], scalar=dnk[:, j:j + 1], in1=hf[:, tt, hsl],
                                                                                                            op0=ALU.mult, op1=ALU.add),
                                 reads=[("ps", nbk), S_("dn", h % 2), S_("hf", tt, h)], writes=[S_("hs", j, h)])
                    ef = Ef[h % 2]
                    col = seg * 16 + d * 8 + h
                    P.op("scalar", lambda e, ef=ef, Eb=Eb, PB=PB, col=col: e.activation(out=ef[PB, :], in_=Eb[PB, 0:130], func=AF.Copy, scale=aendb[PB, col:col + 1]),
                         reads=[("ps", ebk), S_("aendb")], writes=[S_("Ef", h % 2)])
                    P.dma("sync", lambda e, ef=ef, PB=PB, col=col: e.dma_start(out=stE_d[:, col, :], in_=ef[PB, :]), reads=[S_("Ef", h % 2)], final=True)
                    P.op("scalar", lambda e, ef=ef, PB=PB, pair=pair, d=d: e.activation(out=Cst[PB, d, pair, :], in_=ef[PB, :], func=AF.Copy, scale=vecs[PB, VC["flag"]:VC["flag"] + 1]),
                         reads=[S_("Ef", h % 2), "vecs"], writes=[S_("Cst", d, h)])
                if d == 1:
                    for j in range(2):
                        tt = 2 * seg + j
                        tsl = slice(tt * 128, (tt + 1) * 128)
                        hsj = hs[:, j, :]
                        hsv = hsj.rearrange("p (h n) -> p h n", h=8)
                        hrd = [S_("hs", j, h) for h in range(8)]
                        P.op("vector", lambda e, hsv=hsv: e.reduce_sum(out=s1, in_=hsv, axis=AX.X), reads=hrd, writes=[S_("s1")])
                        for h in range(8):
                            P.op("scalar", lambda e, hsj=hsj, h=h: e.activation(out=junk2[:, :], in_=hsj[:, h * 128:(h + 1) * 128], func=AF.Square, accum_out=s2[:, h:h + 1]),
                                 reads=hrd, writes=[S_("s2", h), "junk2"])
                        P.op("vector", lambda e: e.tensor_scalar(out=mean_, in0=s1, scalar1=1.0 / 128, scalar2=None, op0=ALU.mult), reads=[S_("s1")], writes=[S_("mean")])
                        P.op("vector", lambda e: e.tensor_tensor(out=s1, in0=mean_, in1=mean_, op=ALU.mult), reads=[S_("mean")], writes=[S_("s1")])
                        P.op("vector", lambda e: e.scalar_tensor_tensor(out=rstd_, in0=s2, scalar=1.0 / 128, in1=s1, op0=ALU.mult, op1=ALU.subtract),
                             reads=[S_("s2", h) for h in range(8)] + [S_("s1")], writes=[S_("rstd")])
                        P.op("vector", lambda e: e.tensor_scalar(out=rstd_, in0=rstd_, scalar1=LN_EPS, scalar2=None, op0=ALU.add), reads=[S_("rstd")], writes=[S_("rstd")])
                        P.op("scalar", lambda e: e.activation(out=rstd_, in_=rstd_, func=AF.Sqrt), reads=[S_("rstd")], writes=[S_("rstd")])
                        P.op("vector", lambda e: e.reciprocal(out=rstd_, in_=rstd_), reads=[S_("rstd")], writes=[S_("rstd")])
                        P.op("vector", lambda e, hsv=hsv: e.tensor_tensor(out=hsv, in0=hsv, in1=mean_.unsqueeze(2).to_broadcast([128, 8, 128]), op=ALU.subtract),
                             reads=hrd + [S_("mean")], writes=hrd)
                        P.op("vector", lambda e, hsv=hsv: e.tensor_tensor(out=hsv, in0=hsv, in1=rstd_.unsqueeze(2).to_broadcast([128, 8, 128]), op=ALU.mult),
                             reads=hrd + [S_("rstd")], writes=hrd)
                        for oh in range(2):
                            b = next_bank((0, 1))
                            for kc in range(8):
                                P.op("tensor", lambda e, b=b, kc=kc, tsl=tsl, oh=oh, wo=wo: e.matmul(
                                    banks[b][:], lhsT=hT[:, kc, tsl], rhs=wo[:, kc, oh * 512:(oh + 1) * 512], start=(kc == 0), stop=(kc == 7)),
                                    reads=[rreso, ("H", kc, tt // 4)], writes=[("ps", b)], chain=True)
                            P.op("scalar", lambda e, b=b: e.activation(out=sg, in_=banks[b][:], func=AF.Sigmoid), reads=[("ps", b)], writes=[S_("sg")])
                            P.op("vector", lambda e, hsj=hsj, oh=oh: e.tensor_tensor(out=hgt[:, oh * 512:(oh + 1) * 512], in0=hsj[:, oh * 512:(oh + 1) * 512], in1=sg, op=ALU.mult),
                                 reads=hrd + [S_("sg")], writes=[S_("hgt", oh)])
                        b = next_bank((6, 7))
                        pv = banks[b][:].bitcast(BF16)
                        for c in range(8):
                            P.op("tensor", lambda e, pv=pv, c=c: e.transpose(out=pv[:, c * 128:(c + 1) * 128], in_=hgt[:, c * 128:(c + 1) * 128], identity=ident_bf),
                                 reads=[S_("hgt", c // 4), "cbf"], writes=[("ps", b)], chain=(c > 0))
                        for c in range(8):
                            P.op("scalar", lambda e, pv=pv, c=c, tsl=tsl: e.activation(out=hnT[:, c, tsl], in_=pv[:, c * 128:(c + 1) * 128], func=AF.Copy,
                                                                                      scale=vecs[:, VC["mlng"] + c:VC["mlng"] + c + 1]),
                                 reads=[("ps", b), "vecs"], writes=[S_("hnT", c, tt)])
        out_proj(i, "ml_out", hnT, lambda kc, hh: [S_("hnT", kc, t_) for t_ in range(4 * hh, 4 * hh + 4)], lambda dc: g1(i, dc))

    P.dma("sync", lambda e: e.dma_start(out=xa[:, 0:4, :], in_=xT_d[:, 0:4, :]), writes=[("xa", c, h) for c in range(4) for h in range(2)])
    P.dma("sync", lambda e: e.dma_start(out=xa[:, 4:8, :], in_=xT_d[:, 4:8, :]), writes=[("xa", c, h) for c in range(4, 8) for h in range(2)])
    posv = sview(0, 32768, F32, "p (c t) -> p c t", c=8)
    P.dma("sync", lambda e: e.dma_start(out=posv, in_=posT_d[:, :, :]), writes=[("S", "pos")])
    layers = cfg["layers"]
    it0 = mod_steps(layers[0])
    for _ in range(16):
        mod_step(*next(it0))
    mod_finish_a(layers[0])
    pend["it"], pend["layer"], pend["done"] = it0, layers[0], False
    for c in range(8):
        for hf in range(2):
            sl = slice(hf * 512, (hf + 1) * 512)
            P.op("vector", lambda e, c=c, sl=sl: e.tensor_tensor(out=xa[:, c, sl], in0=xa[:, c, sl], in1=posv[:, c, sl], op=ALU.add),
                 reads=[("xa", c, hf), ("S", "pos")], writes=[("xa", c, hf)])
            P.op("scalar", lambda e, c=c, sl=sl: e.mul(out=xa[:, c, sl], in_=xa[:, c, sl], mul=ALPHA),
                 reads=[("xa", c, hf)], writes=[("xa", c, hf)])
    P.fence("S")
    modulate(layers[0], 0)
    for li, i in enumerate(layers):
        kind = i % 4
        nxt = layers[li + 1] if li + 1 < len(layers) else None
        steps = mod_steps(nxt) if nxt is not None else iter(())
        if kind != 0 or kind not in cfg["mixers"]:
            finish_pending()
        if kind in cfg["mixers"]:
            if kind == 0:
                mixer_mlstm(i)
            elif kind == 1:
                mixer_conv(i)
            elif kind == 2:
                mixer_pool(i)
            else:
                mixer_fourier(i)
        P.fence("S")
        for _ in ln_phases(i, 0, 0, True, i, 1):
            pass
        lngen = ln_phases(i, 0, 1, True, i, 1)
        if not cfg["mlp"]:
            for _ in lngen:
                pass
        if cfg["mlp"]:
            mlp(i, steps, nxt, lngen)
        else:
            for st in steps:
                mod_step(*st)
            if nxt is not None:
                mod_finish_a(nxt)
                mod_finish_b(nxt)
            layer_norm(i, 1, do_h=(nxt is not None), nexti=nxt, nextj=0)
        P.fence("S")
    for c in range(8):
        P.dma("sync", lambda e, c=c: e.dma_start(out=yT_d[:, c, :], in_=xa[:, c, :]), reads=[("xa", c, 0), ("xa", c, 1)], final=True)
    P.finish()
    P.build()
    return nc


def from_mlstm(i):
    raise NotImplementedError


def make_in_maps(inp):
    inp = {k: np.asarray(v) for k, v in inp.items()}
    wall = build_wall(inp)
    wmodT = np.ascontiguousarray(np.asarray(inp["w_mod"], np.float32).transpose(0, 2, 1).reshape(4 * 6144, 1024))
    cb = const_bf()
    cf = const_f32()
    mixP = mix_mats(True).reshape(6 * 128, UNITC)
    mixS = mix_mats(False).reshape(6 * 128, UNITC)
    pos = grid_pos_embed_np(16)
    G, bi, bf = gate_layout(inp["ml_w_gate"][0], inp["ml_b_gate"][0])
    maps = []
    for core in range(8):
        prompt = core < 4
        vec = np.zeros((128, NV), np.float32)
        for i in range(4):
            bm = np.asarray(inp["b_mod"][i], np.float32).reshape(48, 128).T
            vec[:, VC["bmod"] + i * 48:VC["bmod"] + (i + 1) * 48] = bm
            for j in range(2):
                vec[:, VC["lng"] + (i * 2 + j) * 8:VC["lng"] + (i * 2 + j + 1) * 8] = fm(inp["ln_g"][i, j])
                vec[:, VC["lnb"] + (i * 2 + j) * 8:VC["lnb"] + (i * 2 + j + 1) * 8] = fm(inp["ln_b"][i, j])
        for k in range(3):
            vec[:, VC["convw"] + k * 8:VC["convw"] + (k + 1) * 8] = fm(inp["sc_conv_w"][0, k])
        vec[:, VC["plb"]:VC["plb"] + 8] = fm(np.asarray(inp["pl_b"][0]).reshape(-1))
        vec[:, VC["pls"]:VC["pls"] + 8] = fm(inp["pl_scale"][0])
        vec[:, VC["ftb"]:VC["ftb"] + 8] = fm(inp["ft_b_out"][0])
        vec[:, VC["flag"]] = 0.0 if prompt else 1.0
        vec[:, VC["bflag"]] = 1.0 if prompt else 0.0
        vec[:, VC["gbi"]] = bi
        vec[:, VC["ngbf"]] = -bf
        vec[:, VC["mlng"]:VC["mlng"] + 8] = fm(inp["ml_norm_g"][0])
        st0 = np.zeros((64, 16, 130), np.float32)
        if prompt:
            x = np.asarray(inp["x_prompt"][4 * core:4 * core + 4], np.float32).reshape(1024, 1024)
            posT = np.zeros((128, 8, 1024), np.float32)
            cond = np.asarray(inp["c_ctx"], np.float32)
        else:
            b = core - 4
            x = np.asarray(inp["x_sample"][b], np.float32)
            posT = to_fm3(pos)
            cond = np.asarray(inp["c"][b], np.float32)
            C0 = np.asarray(inp["state_C"][b, 0], np.float32)
            n0 = np.asarray(inp["state_n"][b, 0], np.float32)
            m0 = np.asarray(inp["state_m"][b, 0], np.float32)
            st0[:, :, 0:128] = C0.reshape(16, 64, 128).transpose(1, 0, 2)
            st0[:, :, 128] = n0.reshape(16, 64).T
            for d in range(2):
                vec[32 * d:32 * d + 8, VC["m0"]] = m0[d]
        maps.append({
            "xT": to_fm3(x), "posT": posT, "condb": np.ascontiguousarray(np.broadcast_to(cond[None, :], (128, 1024))),
            "vecs": vec, "wmodT": wmodT, "wall": wall, "mixm": mixP if prompt else mixS, "cbf": cb, "cf": cf,
            "st0": st0, "ngb": np.ascontiguousarray(np.broadcast_to(np.asarray(inp["ml_norm_g"][0], np.float32)[None, :], (128, 1024))),
        })
    return maps


_NC_CACHE = {}


def kernel(**inputs):
    key = "main"
    if key not in _NC_CACHE:
        _NC_CACHE[key] = build_program(CFG)
    nc = _NC_CACHE[key]
    maps = make_in_maps(inputs)
    res = run_bass_kernel_spmd(nc, maps, core_ids=list(range(8)))
    outs = res.results
    y_prompt = np.zeros((16, 256, 1024), np.float32)
    y_sample = np.zeros((4, 1024, 1024), np.float32)
    new_C = np.zeros((16, 1, 2, 8, 64, 128), np.float32)
    new_n = np.zeros((16, 1, 2, 8, 64), np.float32)
    new_m = np.zeros((16, 1, 2, 8), np.float32)
    for core in range(8):
        y = from_fm3(np.asarray(outs[core]["yT"], np.float32))
        if core < 4:
            y_prompt[4 * core:4 * core + 4] = y.reshape(4, 256, 1024)
            stE = np.asarray(outs[core]["stE"], np.float32).reshape(64, 4, 2, 8, 130)
            stm = np.asarray(outs[core]["stm"], np.float32)
            for seg in range(4):
                q = 4 * core + seg
                new_C[q, 0] = stE[:, seg, :, :, 0:128].transpose(1, 2, 0, 3)
                new_n[q, 0] = stE[:, seg, :, :, 128].transpose(1, 2, 0)
                for d in range(2):
                    new_m[q, 0, d] = stm[32 * d:32 * d + 8, seg]
        else:
            y_sample[core - 4] = y
    return (y_prompt, y_sample, new_C, new_n, new_m)
```
